# Optimizing a Trainium2 kernel written in Bass

```python
import math
import jax
import jax.numpy as jnp
from jax import lax
import numpy as np

D_MODEL = 4096
BATCH = 2
SEQ = 8192
DEPTH = 2

MEM_LEN = 256
N_MIXERS = 2
N_POOL_LAYERS = (DEPTH + N_MIXERS - 1) // N_MIXERS
N_NSA_LAYERS = DEPTH // N_MIXERS

POOL_WINDOWS = (2, 4, 8, 16)
POOL_GROUPS = len(POOL_WINDOWS)
POOL_GC = D_MODEL // POOL_GROUPS

HEAD_DIM = 128
NSA_HEADS = D_MODEL // HEAD_DIM
NSA_KV_GROUPS = 4
NSA_HPG = NSA_HEADS // NSA_KV_GROUPS
NSA_BRANCHES = 3
CMP_LEN = 32
CMP_STRIDE = 16
CMP_HIDDEN = 512
SEL_BLOCK = 64
N_SELECT = 16
WINDOW = 512
NSA_Q_BLOCK = 64
Q_WIDTH = NSA_HEADS * HEAD_DIM
KV_WIDTH = NSA_BRANCHES * 2 * NSA_KV_GROUPS * HEAD_DIM
GATE_WIDTH = NSA_BRANCHES * NSA_HEADS
NSA_IN_WIDTH = Q_WIDTH + KV_WIDTH + GATE_WIDTH
FORCED_SCORE = 1e6

XA_HEADS = 4
XA_WIDTH = XA_HEADS * HEAD_DIM

D_FF = 11008
CONV_WIDTH = 3

RMS_EPS = 1e-6
NEG_BIG = -1e30

kernel_name = "hybrid_pool_nsa_memxattn_convffn"


def rmsnorm(x, g):
    xf = x.astype(jnp.float32)
    y = xf * lax.rsqrt(jnp.mean(xf * xf, axis=-1, keepdims=True) + RMS_EPS)
    return (y * g.astype(jnp.float32)).astype(x.dtype)


def alibi_slopes(n):
    return 2.0 ** (-8.0 * jnp.arange(1, n + 1, dtype=jnp.float32) / n)


def masked_softmax(s, mask):
    s = jnp.where(mask, s, NEG_BIG)
    m = jnp.max(s, axis=-1, keepdims=True)
    e = jnp.exp(s - m) * mask
    return e / jnp.maximum(jnp.sum(e, axis=-1, keepdims=True), 1e-30)


def pool_mixer(h, w, scale):
    B, S, D = h.shape
    hf = h.astype(jnp.float32)
    csum = jnp.cumsum(hf, axis=1)
    t1 = jnp.arange(1, S + 1, dtype=jnp.float32)
    parts = []
    for g, win in enumerate(POOL_WINDOWS):
        c = csum[..., g * POOL_GC:(g + 1) * POOL_GC]
        c_prev = jnp.pad(c, ((0, 0), (win, 0), (0, 0)))[:, :S]
        cnt = jnp.minimum(t1, float(win))
        parts.append((c - c_prev) / cnt[None, :, None])
    pooled = jnp.concatenate(parts, axis=-1)
    d = (pooled - hf).astype(h.dtype).reshape(B, S, POOL_GROUPS, POOL_GC)
    y = jnp.einsum('bsgc,gcd->bsgd', d, w).reshape(B, S, D)
    return y * scale


def compress_blocks(raw, pos, w1, b1, w2):
    B, S, G, dh = raw.shape
    ratio = CMP_LEN // CMP_STRIDE
    nch = S // CMP_STRIDE
    nc = nch - ratio + 1
    chunks = raw.reshape(B, nch, CMP_STRIDE, G, dh)
    blocks = jnp.concatenate([chunks[:, r:r + nc] for r in range(ratio)], axis=2)
    blocks = blocks + pos[None, None, :, None, :]
    flat = blocks.transpose(0, 1, 3, 2, 4).reshape(B, nc, G, CMP_LEN * dh)
    hid = jax.nn.gelu(flat @ w1 + b1)
    return hid @ w2


def nsa_mixer(h, w_in, w_out, cmp_pos, cmp_w1, cmp_b1, cmp_w2):
    B, S, _ = h.shape
    G, J, dh = NSA_KV_GROUPS, NSA_HPG, HEAD_DIM
    proj = h @ w_in
    q = proj[..., :Q_WIDTH].reshape(B, S, G, J, dh)
    kv = proj[..., Q_WIDTH:Q_WIDTH + KV_WIDTH].reshape(B, S, NSA_BRANCHES, 2, G, dh)
    gates = jax.nn.sigmoid(proj[..., Q_WIDTH + KV_WIDTH:].astype(jnp.float32)).reshape(B, S, NSA_BRANCHES, G, J)

    k_cmp = compress_blocks(kv[:, :, 0, 0], cmp_pos[0], cmp_w1[0], cmp_b1[0], cmp_w2[0])
    v_cmp = compress_blocks(kv[:, :, 0, 1], cmp_pos[1], cmp_w1[1], cmp_b1[1], cmp_w2[1])
    nc = k_cmp.shape[1]
    ns = S // SEL_BLOCK
    n_top = min(N_SELECT, ns)
    k_sel = kv[:, :, 1, 0].reshape(B, ns, SEL_BLOCK, G, dh).transpose(0, 3, 1, 2, 4)
    v_sel = kv[:, :, 1, 1].reshape(B, ns, SEL_BLOCK, G, dh).transpose(0, 3, 1, 2, 4)
    k_win = jnp.pad(kv[:, :, 2, 0], ((0, 0), (WINDOW, 0), (0, 0), (0, 0)))
    v_win = jnp.pad(kv[:, :, 2, 1], ((0, 0), (WINDOW, 0), (0, 0), (0, 0)))

    slope_b = alibi_slopes(NSA_HEADS).reshape(G, J)[None, :, :, None, None]
    scale = HEAD_DIM ** -0.5
    cmp_start = jnp.arange(nc) * CMP_STRIDE
    cmp_end = (cmp_start + CMP_LEN - 1).astype(jnp.float32)
    sel_start = jnp.arange(ns) * SEL_BLOCK
    overlap = ((cmp_start[:, None] < sel_start[None, :] + SEL_BLOCK)
               & (cmp_start[:, None] + CMP_LEN > sel_start[None, :])).astype(jnp.float32)
    blk_ids = jnp.arange(ns)
    bi = jnp.arange(B)[:, None, None, None]
    gi = jnp.arange(G)[None, :, None, None]
    sel_off = jnp.arange(SEL_BLOCK)
    win_off = jnp.arange(WINDOW + NSA_Q_BLOCK)
    QB = NSA_Q_BLOCK

    def block(qb):
        start = qb * QB
        t = start + jnp.arange(QB)
        tf = t.astype(jnp.float32)
        q_blk = lax.dynamic_slice_in_dim(q, start, QB, axis=1)
        g_blk = lax.dynamic_slice_in_dim(gates, start, QB, axis=1)
        d_c = tf[:, None] - cmp_end[None, :]
        s_c = jnp.einsum('bqgjd,bcgd->bgjqc', q_blk, k_cmp).astype(jnp.float32) * scale - slope_b * d_c
        p_c = masked_softmax(s_c, d_c >= 0)
        o_c = jnp.einsum('bgjqc,bcgd->bqgjd', p_c.astype(v_cmp.dtype), v_cmp)
        score = jnp.einsum('bgqc,cn->bgqn', jnp.sum(p_c, axis=2), overlap)
        cur = t // SEL_BLOCK
        forced = (blk_ids[None, :] == 0) | (blk_ids[None, :] == cur[:, None]) | (blk_ids[None, :] == cur[:, None] - 1)
        future = blk_ids[None, :] > cur[:, None]
        score = jnp.where(future, -1.0, jnp.where(forced, FORCED_SCORE, score))
        _, idx = lax.top_k(score, n_top)
        kg = k_sel[bi, gi, idx].reshape(B, G, QB, n_top * SEL_BLOCK, dh)
        vg = v_sel[bi, gi, idx].reshape(B, G, QB, n_top * SEL_BLOCK, dh)
        pos = (idx[..., None] * SEL_BLOCK + sel_off).reshape(B, G, QB, n_top * SEL_BLOCK)
        d_s = (t[None, None, :, None] - pos).astype(jnp.float32)[:, :, None]
        s_s = jnp.einsum('bqgjd,bgqkd->bgjqk', q_blk, kg).astype(jnp.float32) * scale - slope_b * d_s
        p_s = masked_softmax(s_s, d_s >= 0)
        o_s = jnp.einsum('bgjqk,bgqkd->bqgjd', p_s.astype(vg.dtype), vg)
        kw = lax.dynamic_slice_in_dim(k_win, start, WINDOW + QB, axis=1)
        vw = lax.dynamic_slice_in_dim(v_win, start, WINDOW + QB, axis=1)
        pos_w = start - WINDOW + win_off
        d_w = t[:, None] - pos_w[None, :]
        mask_w = (d_w >= 0) & (d_w < WINDOW) & (pos_w[None, :] >= 0)
        s_w = jnp.einsum('bqgjd,bkgd->bgjqk', q_blk, kw).astype(jnp.float32) * scale - slope_b * d_w.astype(jnp.float32)
        p_w = masked_softmax(s_w, mask_w)
        o_w = jnp.einsum('bgjqk,bkgd->bqgjd', p_w.astype(vw.dtype), vw)
        g = g_blk[..., None]
        o = g[:, :, 0] * o_c + g[:, :, 1] * o_s + g[:, :, 2] * o_w
        return o.astype(h.dtype)

    o = lax.map(block, jnp.arange(S // QB))
    o = o.transpose(1, 0, 2, 3, 4, 5).reshape(B, S, Q_WIDTH)
    return o @ w_out


def mem_cross_attn(h, memn, wq, wkv, wo):
    B, S, _ = h.shape
    M = memn.shape[1]
    q = (h @ wq).reshape(B, S, XA_HEADS, HEAD_DIM)
    kv = (memn @ wkv).reshape(B, M, 2, XA_HEADS, HEAD_DIM)
    k, v = kv[:, :, 0], kv[:, :, 1]
    s = jnp.einsum('bshd,bmhd->bhsm', q, k).astype(jnp.float32) * (HEAD_DIM ** -0.5)
    p = jax.nn.softmax(s, axis=-1)
    o = jnp.einsum('bhsm,bmhd->bshd', p.astype(v.dtype), v).reshape(B, S, XA_WIDTH)
    return o @ wo


def conv_ffn(h, w_gu, conv_w, conv_b, w_down):
    gu = h @ w_gu
    gate, up = gu[..., :D_FF], gu[..., D_FF:]
    gp = jnp.pad(gate, ((0, 0), (CONV_WIDTH - 1, 0), (0, 0)))
    gate = conv_w[0] * gp[:, :-2] + conv_w[1] * gp[:, 1:-1] + conv_w[2] * gp[:, 2:] + conv_b
    return (jax.nn.silu(gate) * up) @ w_down


def setup_inputs(seed: int = 0) -> dict:
    key = jax.random.key(seed)
    ks = jax.random.split(key, 24)
    f32 = jnp.float32

    def nrm(k, shape, s):
        return jax.random.normal(k, shape, f32) * s

    return {
        "x": nrm(ks[0], (BATCH, SEQ, D_MODEL), 1.0),
        "mem": nrm(ks[1], (BATCH, MEM_LEN, D_MODEL), 1.0),
        "ln_mix": 1.0 + nrm(ks[2], (DEPTH, 2, D_MODEL), 0.05),
        "ln_xa": 1.0 + nrm(ks[3], (DEPTH, 2, D_MODEL), 0.05),
        "ln_ffn": 1.0 + nrm(ks[4], (DEPTH, 2, D_MODEL), 0.05),
        "mem_norm": 1.0 + nrm(ks[5], (D_MODEL,), 0.05),
        "pool_w": nrm(ks[6], (N_POOL_LAYERS, POOL_GROUPS, POOL_GC, POOL_GC), POOL_GC ** -0.5),
        "pool_scale": 1.0 + nrm(ks[7], (N_POOL_LAYERS, D_MODEL), 0.1),
        "nsa_w_in": nrm(ks[8], (N_NSA_LAYERS, D_MODEL, NSA_IN_WIDTH), D_MODEL ** -0.5),
        "nsa_w_out": nrm(ks[9], (N_NSA_LAYERS, Q_WIDTH, D_MODEL), Q_WIDTH ** -0.5),
        "nsa_cmp_pos": nrm(ks[10], (N_NSA_LAYERS, 2, CMP_LEN, HEAD_DIM), 0.5),
        "nsa_cmp_w1": nrm(ks[11], (N_NSA_LAYERS, 2, CMP_LEN * HEAD_DIM, CMP_HIDDEN), (CMP_LEN * HEAD_DIM) ** -0.5),
        "nsa_cmp_b1": nrm(ks[12], (N_NSA_LAYERS, 2, CMP_HIDDEN), 0.01),
        "nsa_cmp_w2": nrm(ks[13], (N_NSA_LAYERS, 2, CMP_HIDDEN, HEAD_DIM), CMP_HIDDEN ** -0.5),
        "xa_wq": nrm(ks[14], (DEPTH, D_MODEL, XA_WIDTH), D_MODEL ** -0.5),
        "xa_wkv": nrm(ks[15], (DEPTH, D_MODEL, 2 * XA_WIDTH), D_MODEL ** -0.5),
        "xa_wo": nrm(ks[16], (DEPTH, XA_WIDTH, D_MODEL), XA_WIDTH ** -0.5),
        "ffn_w_gu": nrm(ks[17], (DEPTH, D_MODEL, 2 * D_FF), D_MODEL ** -0.5),
        "ffn_conv_w": nrm(ks[18], (DEPTH, CONV_WIDTH, D_FF), CONV_WIDTH ** -0.5),
        "ffn_conv_b": nrm(ks[19], (DEPTH, D_FF), 0.01),
        "ffn_w_down": nrm(ks[20], (DEPTH, D_FF, D_MODEL), D_FF ** -0.5),
    }


def reference(x, mem, ln_mix, ln_xa, ln_ffn, mem_norm, pool_w, pool_scale, nsa_w_in, nsa_w_out,
              nsa_cmp_pos, nsa_cmp_w1, nsa_cmp_b1, nsa_cmp_w2, xa_wq, xa_wkv, xa_wo,
              ffn_w_gu, ffn_conv_w, ffn_conv_b, ffn_w_down):
    memn = rmsnorm(mem, mem_norm)
    h = x
    for i in range(DEPTH):
        j = i // N_MIXERS
        a = rmsnorm(h, ln_mix[i, 0])
        if i % N_MIXERS == 0:
            a = pool_mixer(a, pool_w[j], pool_scale[j])
        else:
            a = nsa_mixer(a, nsa_w_in[j], nsa_w_out[j], nsa_cmp_pos[j], nsa_cmp_w1[j],
                          nsa_cmp_b1[j], nsa_cmp_w2[j])
        h = h + rmsnorm(a, ln_mix[i, 1])
        c = mem_cross_attn(rmsnorm(h, ln_xa[i, 0]), memn, xa_wq[i], xa_wkv[i], xa_wo[i])
        h = h + rmsnorm(c, ln_xa[i, 1])
        f = conv_ffn(rmsnorm(h, ln_ffn[i, 0]), ffn_w_gu[i], ffn_conv_w[i], ffn_conv_b[i], ffn_w_down[i])
        h = h + rmsnorm(f, ln_ffn[i, 1])
    return h
```

```python
import contextlib
import numpy as np
import concourse.bass as bass
import concourse.mybir as mybir
from concourse.bass_utils import run_bass_kernel_spmd

F32 = mybir.dt.float32
BF16 = mybir.dt.bfloat16
AF = mybir.ActivationFunctionType
ALU = mybir.AluOpType
RMS_EPS = 1e-6
POOL_WINDOWS = (2, 4, 8, 16)


class Cfg:
    def __init__(s, D=4096, S=8192, DFF=11008, T=256, depth=2, M=256):
        s.D, s.S, s.DFF, s.T, s.depth, s.M = D, S, DFF, T, depth, M
        s.DC = D // 128
        s.FC = DFF // 128
        s.NT = S // T
        s.GC = D // 4
        s.GCC = s.GC // 128
        s.H = D // 128
        s.G = 4
        s.J = s.H // 4
        s.QW = D
        s.KVW = 3 * 2 * 4 * 128
        s.GW = 3 * s.H
        s.INW = s.QW + s.KVW + s.GW
        s.NCMP = S // 16 - 1
        s.NS = S // 64
        s.HALO = 16


class Eng:
    def __init__(s, name, h, sem):
        s.name, s.h, s.sem, s.cnt, s.known = name, h, sem, 0, {}
        s.dsems = []
        s.dcnt = 0


class Buf:
    __slots__ = ("w", "r", "name")
    ALL = []

    def __init__(s, name=""):
        s.w = None
        s.r = {}
        s.name = name
        Buf.ALL.append(s)


class KB:
    def __init__(s, nc, es):
        s.nc = nc
        s.es = es
        mk = lambda n: es.enter_context(nc.semaphore(n))
        s.PE = Eng("pe", nc.tensor, mk("s_pe"))
        s.ACT = Eng("act", nc.scalar, mk("s_act"))
        s.DVE = Eng("dve", nc.vector, mk("s_dve"))
        s.POOL = Eng("pool", nc.gpsimd, mk("s_pool"))
        s.SP = Eng("sp", nc.sync, mk("s_sp"))
        s.SP.dsems = [mk(f"d_sp{i}") for i in range(8)]
        s.POOL.dsems = [mk(f"d_pl{i}") for i in range(8)]
        s.ACT.dsems = [mk(f"d_ac{i}") for i in range(4)]
        s.BA = mk("bar_a")
        s.BB = mk("bar_b")
        s.ep = 0
        Buf.ALL.clear()

    def _sync(s, E, rd, wr):
        toks = []
        for b in rd:
            if b.w is not None:
                toks.append(b.w)
        for b in wr:
            if b.w is not None:
                toks.append(b.w)
            toks.extend(b.r.values())
        for (sem, val, src) in toks:
            if E is s.PE and src is s.PE:
                continue
            k = id(sem)
            if E.known.get(k, 0) >= val:
                continue
            E.h.wait_ge(sem, val)
            E.known[k] = val

    def _mark(s, tok, rd, wr):
        k = id(tok[0])
        for b in rd:
            b.r[k] = tok
        for b in wr:
            b.w = tok
            b.r = {}

    def op(s, E, fn, rd=(), wr=()):
        s._sync(E, rd, wr)
        ins = fn(E.h)
        E.cnt += 1
        ins.then_inc(E.sem, 1)
        s._mark((E.sem, E.cnt, E), rd, wr)

    def dma(s, out, in_, rd=(), wr=(), q=None, **kw):
        Q = q or s.SP
        K = len(Q.dsems)
        i = Q.dcnt
        sem = Q.dsems[i % K]
        if i >= K:
            v = 16 * (i // K)
            if Q.known.get(id(sem), 0) < v:
                Q.h.wait_ge(sem, v)
                Q.known[id(sem)] = v
        s._sync(Q, rd, wr)
        Q.h.dma_start(out=out, in_=in_, **kw).then_inc(sem, 16)
        Q.dcnt += 1
        tok = (sem, 16 * (i // K + 1), None)
        s._mark(tok, rd, wr)
        return tok

    def epoch(s, ep_val=None):
        static = ep_val is None
        if static:
            ep_val = s.ep
            s.ep += 1
        engs = (s.PE, s.ACT, s.DVE, s.POOL, s.SP)
        for E in engs:
            if E.cnt:
                E.h.wait_ge(E.sem, E.cnt)
            K = len(E.dsems)
            for j in range(min(K, E.dcnt)):
                E.h.wait_ge(E.dsems[j], 16 * ((E.dcnt - 1 - j) // K + 1))
            E.h.sem_inc(s.BA, 1)
        for E in engs:
            E.h.wait_ge(s.BA, 5 * (ep_val + 1))
            E.h.sem_clear(E.sem)
            K = len(E.dsems)
            for j in range(min(K, E.dcnt)):
                E.h.sem_clear(E.dsems[j])
            E.h.sem_inc(s.BB, 1)
        for E in engs:
            E.h.wait_ge(s.BB, 5 * (ep_val + 1))
            E.cnt = 0
            E.known = {}
            E.dcnt = 0
        for b in Buf.ALL:
            b.w = None
            b.r = {}

    def barrier(s):
        s.epoch()

    def loop(s, n0, n1, body):
        if n1 <= n0:
            return
        s.epoch()
        ep0 = s.ep
        with s.nc.Fori(n0, n1) as i:
            body(i)
            s.epoch(ep0 + (i - n0))
        s.ep = ep0 + (n1 - n0)

    def finish(s):
        for Q in (s.SP, s.POOL, s.ACT):
            K = len(Q.dsems)
            for j in range(min(K, Q.dcnt)):
                n = (Q.dcnt - 1 - j) // K + 1
                Q.h.wait_ge(Q.dsems[j], 16 * n)


def build(cfg, n_layers=None, debug_out=None):
    c = cfg
    D, S, DFF, T, DC, FC, M = c.D, c.S, c.DFF, c.T, c.DC, c.FC, c.M
    HALO = c.HALO
    TH = T + HALO
    nc = bass.Bass("TRN2", target_bir_lowering=False)
    es = contextlib.ExitStack()
    kb = KB(nc, es)
    PE, ACT, DVE, POOL, SP = kb.PE, kb.ACT, kb.DVE, kb.POOL, kb.SP
    depth = c.depth if n_layers is None else n_layers

    def din(name, shape, dt=F32):
        return nc.dram_tensor(name, list(shape), dt, kind="ExternalInput").ap()

    def dscr(name, shape, dt):
        return nc.dram_tensor(name, list(shape), dt, kind="Internal").ap()

    xT = din("xT", [D, S])
    memT = din("memT", [D, M])
    NV = vec_layout(c)["_n"]
    VL = vec_layout(c)
    vecs_d = din("vecs", [128, NV])
    consts_d = din("consts", [128, const_layout(c)["_n"]])
    CL = const_layout(c)
    outT = nc.dram_tensor("outT", [D, S], F32, kind="ExternalOutput").ap()
    hT = dscr("hT", [D, S], F32)

    wf = {}
    wb = {}
    wshapes = {}
    wshapes["pool_w"] = [4 * c.GC, c.GC]
    for l in range(c.depth):
        wshapes[f"xa_wq{l}"] = [D, 512]
        wshapes[f"xa_wkv{l}"] = [D, 1024]
        wshapes[f"xa_wo{l}"] = [512, D]
        wshapes[f"w_gu{l}"] = [D, 2 * DFF]
        wshapes[f"w_down{l}"] = [DFF, D]
    if c.depth > 1:
        wshapes["nsa_w_in"] = [D, c.INW]
        wshapes["nsa_w_out"] = [D, D]
        wshapes["cmp_w1"] = [2 * 4096, 512]
        wshapes["cmp_w2"] = [2 * 512, 128]
    if c.depth > 1:
        cmp_posT = din("cmp_posT", [2, 128, 32])
    for k, shp in wshapes.items():
        wf[k] = din(k, shp)
        wb[k] = dscr(k + "_b", shp, BF16)

    def sb(name, shape, dt):
        return es.enter_context(nc.sbuf_tensor(name, list(shape), dt))

    h_sb = sb("h_sb", [128, DC, TH], F32)
    f_sb = sb("f_sb", [128, DC, T], F32)
    _CCH = (c.NCMP + 127) // 128
    _NWT = 512 // 128 + T // 128 + 1
    ACTN = max(FC * T, 3 * c.GCC * TH * 2 + c.GCC * T, DC * M * 3, DC * T)
    if c.depth > 1:
        ACTN = max(ACTN, c.H * T + 16 * T + 2 * (T // 128) * 512, S + 2048 + 3072, 2 * S + _CCH * (c.NS + 1) + 64 + T * (1 + 3 + _NWT + 1 + 4))
    ACTN = (ACTN + 63) // 64 * 64
    act_sb = sb("act_sb", [128, ACTN], BF16)
    NWS = 2
    WSZ = 8192
    w_sb = [sb(f"w_sb{i}", [128, WSZ], BF16) for i in range(NWS)]
    vecs = sb("vecs_sb", [128, NV], F32)
    consts = sb("consts_sb", [128, CL["_n"]], F32)
    sq_sb = [sb(f"sq{i}", [128, TH], F32) for i in range(2)]
    rstd_sb = sb("rstd", [128, TH], F32)
    tmp_sb = [sb(f"tmp{i}", [128, TH], F32) for i in range(4)]
    graw_sb = [sb(f"graw{i}", [128, T + 2], F32) for i in range(2)]
    carry_sb = sb("carry", [128, FC, 2], F32)
    sil_sb = [sb(f"sil{i}", [128, T], F32) for i in range(2)]
    ones_bf = sb("ones_bf", [128, 128], BF16)
    q_sb = sb("q_sb", [128, 4, T], BF16)
    kT_sb = sb("kT_sb", [128, 4, M], BF16)
    v_sb = sb("v_sb", [128, M // 128, 512], BF16)
    p_sb = [sb(f"p_sb{i}", [128, T], BF16) for i in range(2)]
    o_sb = sb("o_sb", [128, 4, T], BF16)

    ps = [es.enter_context(nc.psum_tensor(f"ps{i}", [128, 512], F32)) for i in range(8)]

    B = lambda n: Buf(n)
    b_h, b_f, b_act, b_vecs, b_consts = B("h"), B("f"), B("act"), B("vecs"), B("consts")
    b_w = [B(f"w{i}") for i in range(NWS)]
    b_sq = [B("sq0"), B("sq1")]
    b_rstd = B("rstd")
    b_tmp = [B(f"tmp{i}") for i in range(4)]
    b_graw = [B("graw0"), B("graw1")]
    b_carry = B("carry")
    b_sil = [B("sil0"), B("sil1")]
    b_ones = B("ones")
    b_q, b_kT, b_v, b_o = B("q"), B("kT"), B("v"), B("o")
    b_p = [B("p0"), B("p1")]
    b_ps = [B(f"ps{i}") for i in range(8)]
    b_hT = B("hT")
    b_wb = {k: B("wb_" + k) for k in wb}

    ones_f = consts[:, CL["ones"]:CL["ones"] + 128]

    state = {"ps": 0, "w": 0, "sq": 0, "tmp": 0, "p": 0, "acc": 0, "e": 0, "st": 0}

    def next_ps():
        i = state["ps"]
        state["ps"] = (i + 1) % 5
        return ps[2 + i], b_ps[2 + i]

    def next_acc():
        i = state["acc"]
        state["acc"] = i ^ 1
        return ps[i], b_ps[i]

    def next_w():
        i = state["w"]
        state["w"] = (i + 1) % NWS
        return w_sb[i], b_w[i]

    kb.dma(vecs[:], vecs_d[:, :], wr=[b_vecs])
    kb.dma(consts[:], consts_d[:, :], wr=[b_consts])
    kb.op(DVE, lambda e: e.memset(ones_bf[:], 1.0), wr=[b_ones])

    def cast_weight(k):
        src, dst = wf[k], wb[k]
        R, Ccols = wshapes[k]
        tot = R * Ccols
        W = 2048
        while tot % W:
            W //= 2
        rows = tot // W
        s2 = src.rearrange("r c -> (r c)").rearrange("(a b) -> a b", b=W)
        d2 = dst.rearrange("r c -> (r c)").rearrange("(a b) -> a b", b=W)
        step = 2048
        for r0 in range(0, rows, step):
            r1 = min(rows, r0 + step)
            kb.dma(d2[r0:r1, :], s2[r0:r1, :], wr=[b_wb[k]], q=POOL)

    TMK = {}
    wt_m = {}
    b_wt = {}
    for k, (R_, C_) in wshapes.items():
        if k.startswith(("w_gu", "w_down", "xa_wq", "xa_wo")) or k in ("nsa_w_out", "pool_w", "nsa_w_in"):
            ncols_tm = C_ if k != "nsa_w_in" else (c.QW + c.KVW)
            if ncols_tm % 256 == 0 and R_ % 128 == 0:
                TMK[k] = (ncols_tm // 256, R_ // 128)
                wt_m[k] = dscr(k + "_t", [ncols_tm // 256, 128, R_ // 128, 256], BF16)
                b_wt[k] = B("wt_" + k)

    def rearr_weight(k):
        NB_, KC_ = TMK[k]
        for blk in range(NB_):
            kb.dma(wt_m[k][blk], wb[k][:, blk * 256:(blk + 1) * 256].rearrange("(kc p) n -> p kc n", p=128),
                   rd=[b_wb[k]], wr=[b_wt[k]])

    cast_order = ["pool_w"]
    for l in range(depth):
        cast_order += [f"xa_wkv{l}", f"xa_wq{l}", f"xa_wo{l}", f"w_gu{l}", f"w_down{l}"]
        if l == 0 and depth > 1:
            cast_order += ["nsa_w_in", "cmp_w1", "cmp_w2", "nsa_w_out"]
    for k in cast_order:
        if k in wb:
            cast_weight(k)
    for k in cast_order:
        if k in TMK:
            rearr_weight(k)

    def vcol(name, idx=0, n=1):
        o = VL[name] + idx
        return vecs[:, o:o + n]

    def rms_stats(chunks, ncols, rd):
        pt, bpt = next_ps()
        n = len(chunks)
        for i, ap in enumerate(chunks):
            j = state["sq"]
            state["sq"] ^= 1
            kb.op(ACT, lambda e, ap=ap, j=j: e.activation(out=sq_sb[j][:, :ncols], in_=ap, func=AF.Square),
                  rd=rd, wr=[b_sq[j]])
            kb.op(PE, lambda e, i=i, j=j: e.matmul(pt[:, :ncols], ones_f, sq_sb[j][:, :ncols],
                                                   start=(i == 0), stop=(i == n - 1)),
                  rd=[b_sq[j], b_consts], wr=[bpt])
        kb.op(DVE, lambda e: e.tensor_scalar(out=rstd_sb[:, :ncols], in0=pt[:, :ncols], scalar1=1.0 / D,
                                             scalar2=RMS_EPS, op0=ALU.mult, op1=ALU.add),
              rd=[bpt], wr=[b_rstd])
        kb.op(ACT, lambda e: e.activation(out=rstd_sb[:, :ncols], in_=rstd_sb[:, :ncols], func=AF.Sqrt),
              rd=[b_rstd], wr=[b_rstd])
        kb.op(DVE, lambda e: e.reciprocal(out=rstd_sb[:, :ncols], in_=rstd_sb[:, :ncols]),
              rd=[b_rstd], wr=[b_rstd])

    def a_chunk(cidx):
        return a_bf[:, cidx * T:(cidx + 1) * T]

    a_bf = f_sb[:].rearrange("p c t -> p (c t)").bitcast(BF16)

    def norm_to_a(gname, src_fn, rd_src):
        rms_stats([src_fn(ci) for ci in range(DC)], T, rd_src)
        for ci in range(DC):
            kb.op(DVE, lambda e, ci=ci: e.scalar_tensor_tensor(out=a_chunk(ci), in0=src_fn(ci),
                                                                scalar=vcol(gname, ci), in1=rstd_sb[:, :T],
                                                                op0=ALU.mult, op1=ALU.mult),
                  rd=rd_src + [b_rstd, b_vecs], wr=[b_f])

    def h_c(ci):
        return h_sb[:, ci, HALO:HALO + T]

    def residual_add(gname):
        rms_stats([f_sb[:, ci, :] for ci in range(DC)], T, [b_f])
        for ci in range(DC):
            k = state["tmp"]
            state["tmp"] = (k + 1) % 4
            kb.op(DVE, lambda e, ci=ci, k=k: e.scalar_tensor_tensor(out=tmp_sb[k][:, :T], in0=f_sb[:, ci, :],
                                                                     scalar=vcol(gname, ci), in1=rstd_sb[:, :T],
                                                                     op0=ALU.mult, op1=ALU.mult),
                  rd=[b_f, b_rstd, b_vecs], wr=[b_tmp[k]])
            kb.op(POOL, lambda e, ci=ci, k=k: e.tensor_tensor(out=h_c(ci), in0=h_c(ci), in1=tmp_sb[k][:, :T],
                                                              op=ALU.add),
                  rd=[b_tmp[k], b_h], wr=[b_h])

    def load_w(key, r0, nr, c0, ncol):
        wt, bw = next_w()
        assert nr * ncol <= WSZ
        dst = wt[:, :nr * ncol].rearrange("p (k n) -> p k n", n=ncol)
        if key in TMK and ncol == 256 and c0 % 256 == 0 and r0 % 128 == 0 and c0 // 256 < TMK[key][0]:
            src = wt_m[key][c0 // 256][:, r0 // 128:r0 // 128 + nr, :]
            kb.dma(dst, src, rd=[b_wt[key]], wr=[bw])
        else:
            src = wb[key][r0:r0 + nr * 128, c0:c0 + ncol].rearrange("(k p) n -> p k n", p=128)
            kb.dma(dst, src, rd=[b_wb[key]], wr=[bw])
        return dst, bw

    def proj(key, in_fn, rd_in, KC, cols, evac, r0=0, nb=256):
        i = 0
        while i < len(cols):
            grp = [cols[i]]
            while len(grp) * 128 < nb and i + len(grp) < len(cols) and cols[i + len(grp)] == grp[-1] + 128:
                grp.append(cols[i + len(grp)])
            ncol = len(grp) * 128
            kstep = max(1, WSZ // ncol)
            pts = [next_ps() for _ in grp]
            for k0 in range(0, KC, kstep):
                k1 = min(KC, k0 + kstep)
                wt, bw = load_w(key, r0 + k0 * 128, k1 - k0, grp[0], ncol)
                for gi in range(len(grp)):
                    pt, bpt = pts[gi]
                    for k in range(k0, k1):
                        kb.op(PE, lambda e, pt=pt, wt=wt, k=k, k0=k0, gi=gi: e.matmul(
                            pt[:, :T], wt[:, k - k0, gi * 128:(gi + 1) * 128], in_fn(k),
                            start=(k == 0), stop=(k == KC - 1)),
                              rd=[bw] + rd_in, wr=[bpt])
            for gi in range(len(grp)):
                evac(i + gi, pts[gi][0], pts[gi][1])
            i += len(grp)

    def xa_prepare(l):
        mem_f = act_sb[:, :DC * M * 2].bitcast(F32).rearrange("p (c m) -> p c m", m=M)
        memn = act_sb[:, DC * M * 2: DC * M * 3].rearrange("p (c m) -> p c m", m=M)
        kb.dma(mem_f, memT.rearrange("(c p) m -> p c m", p=128), wr=[b_act])
        rms_stats([mem_f[:, ci, :] for ci in range(DC)], M, [b_act])
        for ci in range(DC):
            kb.op(DVE, lambda e, ci=ci: e.scalar_tensor_tensor(out=memn[:, ci, :], in0=mem_f[:, ci, :],
                                                                scalar=vcol("mem_norm", ci), in1=rstd_sb[:, :M],
                                                                op0=ALU.mult, op1=ALU.mult),
                  rd=[b_act, b_rstd, b_vecs], wr=[b_act])
        key = f"xa_wkv{l}"
        for hh in range(4):
            pt, bpt = next_ps()
            kstep = WSZ // 128
            for k0 in range(0, DC, kstep):
                k1 = min(DC, k0 + kstep)
                wt, bw = load_w(key, k0 * 128, k1 - k0, hh * 128, 128)
                for k in range(k0, k1):
                    kb.op(PE, lambda e, pt=pt, wt=wt, k=k, k0=k0: e.matmul(pt[:, :M], wt[:, k - k0, :], memn[:, k, :],
                                                                          start=(k == 0), stop=(k == DC - 1)),
                          rd=[bw, b_act], wr=[bpt])
            kb.op(ACT, lambda e, pt=pt, hh=hh: e.activation(out=kT_sb[:, hh, :], in_=pt[:, :M], func=AF.Copy),
                  rd=[bpt], wr=[b_kT])
        for mc in range(M // 128):
            pt, bpt = next_ps()
            kstep = WSZ // 512
            for k0 in range(0, DC, kstep):
                k1 = min(DC, k0 + kstep)
                wt, bw = load_w(key, k0 * 128, k1 - k0, 512, 512)
                for k in range(k0, k1):
                    kb.op(PE, lambda e, pt=pt, wt=wt, k=k, k0=k0, mc=mc: e.matmul(
                        pt[:, :512], memn[:, k, mc * 128:(mc + 1) * 128], wt[:, k - k0, :],
                        start=(k == 0), stop=(k == DC - 1)),
                          rd=[bw, b_act], wr=[bpt])
            kb.op(ACT, lambda e, pt=pt, mc=mc: e.activation(out=v_sb[:, mc, :], in_=pt[:, :512], func=AF.Copy),
                  rd=[bpt], wr=[b_v])

    def xa_block(l):
        norm_to_a(f"ln_xa{l}_0", h_c, [b_h])
        def evq(i, pt, bpt):
            kb.op(ACT, lambda e: e.activation(out=q_sb[:, i, :], in_=pt[:, :T], func=AF.Copy), rd=[bpt], wr=[b_q])
        proj(f"xa_wq{l}", lambda k: a_chunk(k), [b_f], DC, [hh * 128 for hh in range(4)], evq)
        scale = 128.0 ** -0.5
        for hh in range(4):
            po, bpo = next_ps()
            pss = []
            for mc in range(M // 128):
                pt, bpt = next_ps()
                kb.op(PE, lambda e, pt=pt, mc=mc: e.matmul(pt[:, :T], kT_sb[:, hh, mc * 128:(mc + 1) * 128], q_sb[:, hh, :],
                                                          start=True, stop=True), rd=[b_kT, b_q], wr=[bpt])
                j = state["p"]
                state["p"] ^= 1
                kb.op(ACT, lambda e, pt=pt, j=j: e.activation(out=p_sb[j][:], in_=pt[:, :T], func=AF.Exp, scale=scale),
                      rd=[bpt], wr=[b_p[j]])
                pss.append(j)
            pd, bpd = next_ps()
            for mc, j in enumerate(pss):
                kb.op(PE, lambda e, mc=mc, j=j: e.matmul(po[:, :T], v_sb[:, mc, hh * 128:(hh + 1) * 128], p_sb[j][:],
                                                        start=(mc == 0), stop=(mc == M // 128 - 1)),
                      rd=[b_v, b_p[j]], wr=[bpo])
            for mc, j in enumerate(pss):
                kb.op(PE, lambda e, mc=mc, j=j: e.matmul(pd[:, :T], ones_bf[:], p_sb[j][:],
                                                        start=(mc == 0), stop=(mc == M // 128 - 1)),
                      rd=[b_ones, b_p[j]], wr=[bpd])
            k = state["tmp"]
            state["tmp"] = (k + 1) % 4
            kb.op(DVE, lambda e, k=k: e.reciprocal(out=tmp_sb[k][:, :T], in_=pd[:, :T]), rd=[bpd], wr=[b_tmp[k]])
            kb.op(DVE, lambda e, k=k, hh=hh: e.tensor_tensor(out=o_sb[:, hh, :], in0=po[:, :T], in1=tmp_sb[k][:, :T],
                                                             op=ALU.mult), rd=[bpo, b_tmp[k]], wr=[b_o])
        def evo(i, pt, bpt):
            kb.op(ACT, lambda e: e.activation(out=f_sb[:, i, :], in_=pt[:, :T], func=AF.Copy), rd=[bpt], wr=[b_f])
        proj(f"xa_wo{l}", lambda k: o_sb[:, k, :], [b_o], 4, [m * 128 for m in range(DC)], evo)
        residual_add(f"ln_xa{l}_1")

    def ffn_block(l, first_tile):
        norm_to_a(f"ln_ffn{l}_0", h_c, [b_h])
        if first_tile:
            kb.op(POOL, lambda e: e.memset(carry_sb[:], 0.0), wr=[b_carry])
        act3 = act_sb[:, :FC * T].rearrange("p (j t) -> p j t", t=T)
        key = f"w_gu{l}"
        JB = 2
        for j0 in range(0, FC, JB):
            js = list(range(j0, min(FC, j0 + JB)))
            sil_of = {}

            def ev_gate(i, pt, bpt, js=js, sil_of=sil_of):
                j = js[i]
                g = j % 2
                gr = graw_sb[g]
                kb.op(POOL, lambda e: e.tensor_copy(out=gr[:, 0:2], in_=carry_sb[:, j, :]), rd=[b_carry], wr=[b_graw[g]])
                kb.op(ACT, lambda e: e.activation(out=gr[:, 2:2 + T], in_=pt[:, :T], func=AF.Copy), rd=[bpt], wr=[b_graw[g]])
                kb.op(POOL, lambda e: e.tensor_copy(out=carry_sb[:, j, :], in_=gr[:, T:T + 2]), rd=[b_graw[g]], wr=[b_carry])
                k = state["tmp"]
                state["tmp"] = (k + 1) % 4
                tt = tmp_sb[k]
                cw = lambda r: vcol(f"conv_w{l}_{r}", j)
                kb.op(DVE, lambda e: e.tensor_scalar(out=tt[:, :T], in0=gr[:, 2:2 + T], scalar1=cw(2),
                                                     scalar2=vcol(f"conv_b{l}", j), op0=ALU.mult, op1=ALU.add),
                      rd=[b_graw[g], b_vecs], wr=[b_tmp[k]])
                kb.op(DVE, lambda e: e.scalar_tensor_tensor(out=tt[:, :T], in0=gr[:, 1:1 + T], scalar=cw(1), in1=tt[:, :T],
                                                            op0=ALU.mult, op1=ALU.add),
                      rd=[b_graw[g], b_vecs, b_tmp[k]], wr=[b_tmp[k]])
                kb.op(DVE, lambda e: e.scalar_tensor_tensor(out=tt[:, :T], in0=gr[:, 0:T], scalar=cw(0), in1=tt[:, :T],
                                                            op0=ALU.mult, op1=ALU.add),
                      rd=[b_graw[g], b_vecs, b_tmp[k]], wr=[b_tmp[k]])
                kb.op(ACT, lambda e: e.activation(out=sil_sb[g][:], in_=tt[:, :T], func=AF.Silu),
                      rd=[b_tmp[k]], wr=[b_sil[g]])
                sil_of[j] = g

            def ev_up(i, pt, bpt, js=js, sil_of=sil_of):
                j = js[i]
                g = sil_of[j]
                kb.op(DVE, lambda e: e.tensor_tensor(out=act3[:, j, :], in0=pt[:, :T], in1=sil_sb[g][:], op=ALU.mult),
                      rd=[bpt, b_sil[g]], wr=[b_act])

            proj(key, lambda k: a_chunk(k), [b_f], DC, [j * 128 for j in js], ev_gate)
            proj(key, lambda k: a_chunk(k), [b_f], DC, [DFF + j * 128 for j in js], ev_up)

        def ev_down(i, pt, bpt):
            kb.op(ACT, lambda e: e.activation(out=f_sb[:, i, :], in_=pt[:, :T], func=AF.Copy), rd=[bpt], wr=[b_f])
        proj(f"w_down{l}", lambda k: act3[:, k, :], [b_act], FC, [m * 128 for m in range(DC)], ev_down)
        residual_add(f"ln_ffn{l}_1")

    def pool_block(it):
        GCC = c.GCC
        rms_stats([h_sb[:, ci, :] for ci in range(DC)], TH, [b_h])
        nbf = GCC * TH * 2
        bufs3 = [act_sb[:, i * nbf:(i + 1) * nbf].bitcast(F32).rearrange("p (c t) -> p c t", t=TH) for i in range(3)]
        d_g = act_sb[:, 3 * nbf:3 * nbf + GCC * T].rearrange("p (c t) -> p c t", t=T)
        a0, bA, bB = bufs3
        for g, win in enumerate(POOL_WINDOWS):
            for cc in range(GCC):
                ci = g * GCC + cc
                kb.op(DVE, lambda e, ci=ci, cc=cc: e.scalar_tensor_tensor(out=a0[:, cc, :], in0=h_sb[:, ci, :],
                                                                          scalar=vcol("ln_mix0_0", ci), in1=rstd_sb[:, :TH],
                                                                          op0=ALU.mult, op1=ALU.mult),
                      rd=[b_h, b_rstd, b_vecs], wr=[b_act])
            src_b = a0
            sh = 1
            pp = [bA, bB]
            pi = 0
            while sh < win:
                dst_b = pp[pi]
                pi ^= 1
                lo = 2 * sh - 1
                kb.op(POOL, lambda e, sh=sh, lo=lo, src_b=src_b, dst_b=dst_b: e.tensor_tensor(
                    out=dst_b[:, :, lo:], in0=src_b[:, :, lo:], in1=src_b[:, :, lo - sh:TH - sh], op=ALU.add),
                      rd=[b_act], wr=[b_act])
                src_b = dst_b
                sh *= 2
            ssum = src_b
            kb.op(DVE, lambda e, win=win, ssum=ssum: e.scalar_tensor_tensor(out=d_g[:, :, :], in0=ssum[:, :, HALO:], scalar=1.0 / win,
                                                                             in1=a0[:, :, HALO:], op0=ALU.mult, op1=ALU.subtract),
                  rd=[b_act], wr=[b_act])
            if it == 0:
                ic = consts[:, CL["invc"] + g * 16: CL["invc"] + (g + 1) * 16]
                for cc in range(GCC):
                    k = state["tmp"]
                    state["tmp"] = (k + 1) % 4
                    kb.op(DVE, lambda e, cc=cc, k=k, ssum=ssum: e.tensor_tensor(out=tmp_sb[k][:, :16], in0=ssum[:, cc, HALO:HALO + 16], in1=ic,
                                                                                op=ALU.mult), rd=[b_act, b_consts], wr=[b_tmp[k]])
                    kb.op(DVE, lambda e, cc=cc, k=k: e.tensor_tensor(out=d_g[:, cc, 0:16], in0=tmp_sb[k][:, :16],
                                                                     in1=a0[:, cc, HALO:HALO + 16], op=ALU.subtract),
                          rd=[b_act, b_tmp[k]], wr=[b_act])

            def evp(i, pt, bpt, g=g):
                ci = g * GCC + i
                kb.op(ACT, lambda e: e.activation(out=f_sb[:, ci, :], in_=pt[:, :T], func=AF.Copy,
                                                  scale=vcol("pool_scale", ci)),
                      rd=[bpt, b_vecs], wr=[b_f])
            proj("pool_w", lambda k: d_g[:, k, :], [b_act], GCC, [m * 128 for m in range(GCC)], evp, r0=g * c.GC)
        residual_add("ln_mix0_1")

    def layer0():
        xa_prepare(0)
        xv = xT.rearrange("(c p) s -> p c s", p=128)
        hv = (hT if depth > 1 else outT).rearrange("(c p) s -> p c s", p=128)

        def tile(t0, first):
            if first:
                kb.op(POOL, lambda e: e.memset(h_sb[:, :, 0:HALO], 0.0), wr=[b_h])
                kb.dma(h_sb[:, :, HALO:], xv[:, :, 0:T], wr=[b_h])
            else:
                kb.dma(h_sb[:, :, :], xv[:, :, bass.ds(t0 - HALO, TH)], wr=[b_h])
            pool_block(0 if first else 1)
            xa_block(0)
            ffn_block(0, first)
            kb.dma(hv[:, :, bass.ds(t0, T)], h_sb[:, :, HALO:], rd=[b_h], wr=[b_hT])
        tile(0, True)
        kb.loop(1, c.NT, lambda i: tile(i * T, False))

    def nsa_layer():
        H, G, J, NS, NCMP = c.H, c.G, c.J, c.NS, c.NCMP
        GW = c.GW
        scale = 128.0 ** -0.5
        slopes = [2.0 ** (-8.0 * (i + 1) / H) for i in range(H)]
        NKT = S // 128
        CCH = (NCMP + 127) // 128
        NCP = CCH * 128
        qT = dscr("qT_s", [D, S], BF16)
        kvT = dscr("kvT_s", [16 * 128, S], BF16)
        vtok = [dscr(f"vtok_s{i}", [S, 512], BF16) for i in range(2)]
        gT = dscr("gT_s", [GW, S], F32)
        oT = dscr("oT_s", [D, S], BF16)
        b_qT, b_kvT, b_vtok, b_gT, b_oT = B("qT"), B("kvT"), B("vtok"), B("gT"), B("oT")
        stage = [sb(f"stage{i}", [128, 512], BF16) for i in range(2)]
        b_stage = [B("st0"), B("st1")]
        gst = sb("gst", [128, T], F32)
        b_gst = B("gst")
        kcmpT = sb("kcmpT", [128, G, NCP], BF16)
        vcmp = sb("vcmp", [128, G, CCH, 128], BF16)
        b_kcmp, b_vcmp = B("kcmp"), B("vcmp")
        posT = sb("posT", [128, 2, 32], BF16)
        b_posT = B("posT")
        cbias = sb("cbias", [128, 4], F32)
        b_cbias = B("cbias")
        hv = hT.rearrange("(c p) s -> p c s", p=128)

        def stg():
            i = state["st"]
            state["st"] ^= 1
            return stage[i], b_stage[i]

        kb.barrier()
        xa_prepare(1)
        HT_ = H * T
        q_stage = act_sb[:, 0:HT_].rearrange("p (h t) -> p h t", t=T)
        kv_stage = act_sb[:, HT_:HT_ + 16 * T].rearrange("p (h t) -> p h t", t=T)
        v_stage = act_sb[:, HT_ + 16 * T:HT_ + 16 * T + 2 * (T // 128) * 512].rearrange("p (b a c) -> p b a c", b=2, c=512)
        b_qst, b_kvst, b_vst = B("qst"), B("kvst"), B("vst")

        def b1_tile(t0):
            kb.dma(h_sb[:, :, HALO:], hv[:, :, bass.ds(t0, T)], rd=[b_hT], wr=[b_h])
            norm_to_a("ln_mix1_0", h_c, [b_h])

            def ev_q(i, pt, bpt):
                kb.op(ACT, lambda e: e.activation(out=q_stage[:, i, :], in_=pt[:, :T], func=AF.Copy), rd=[bpt], wr=[b_qst])
            proj("nsa_w_in", lambda k: a_chunk(k), [b_f], DC, [hd * 128 for hd in range(H)], ev_q)
            kb.dma(qT.rearrange("(h p) s -> p h s", p=128)[:, :, bass.ds(t0, T)], q_stage, rd=[b_qst], wr=[b_qT])
            fm_chunks = [(br * 2 + 0) * G + g for br in range(3) for g in range(G)] + [(0 * 2 + 1) * G + g for g in range(G)]

            def ev_kv(i, pt, bpt):
                kb.op(ACT, lambda e: e.activation(out=kv_stage[:, i, :], in_=pt[:, :T], func=AF.Copy), rd=[bpt], wr=[b_kvst])
            proj("nsa_w_in", lambda k: a_chunk(k), [b_f], DC, [c.QW + ch * 128 for ch in fm_chunks], ev_kv)
            kb.dma(kvT.rearrange("(h p) s -> p h s", p=128)[:, :, bass.ds(t0, T)], kv_stage, rd=[b_kvst], wr=[b_kvT])
            for bi, br in enumerate((1, 2)):
                c0 = c.QW + ((br * 2 + 1) * G) * 128
                for ts_ in range(T // 128):
                    pt, bpt = next_ps()
                    kstep = WSZ // 512
                    for k0 in range(0, DC, kstep):
                        k1 = min(DC, k0 + kstep)
                        wt, bw = load_w("nsa_w_in", k0 * 128, k1 - k0, c0, 512)
                        for k in range(k0, k1):
                            kb.op(PE, lambda e, pt=pt, wt=wt, k=k, k0=k0, ts_=ts_: e.matmul(
                                pt[:, :512], a_chunk(k)[:, ts_ * 128:(ts_ + 1) * 128], wt[:, k - k0, :],
                                start=(k == 0), stop=(k == DC - 1)), rd=[bw, b_f], wr=[bpt])
                    kb.op(ACT, lambda e, pt=pt, bi=bi, ts_=ts_: e.activation(out=v_stage[:, bi, ts_, :], in_=pt[:, :512], func=AF.Copy),
                          rd=[bpt], wr=[b_vst])
                kb.dma(vtok[bi][bass.ds(t0, T), :].rearrange("(a p) c -> p a c", p=128), v_stage[:, bi, :, :], rd=[b_vst], wr=[b_vtok])
            pt, bpt = next_ps()
            kstep = WSZ // GW
            c0 = c.QW + c.KVW
            for k0 in range(0, DC, kstep):
                k1 = min(DC, k0 + kstep)
                wt, bw = load_w("nsa_w_in", k0 * 128, k1 - k0, c0, GW)
                for k in range(k0, k1):
                    kb.op(PE, lambda e, pt=pt, wt=wt, k=k, k0=k0: e.matmul(pt[:GW, :T], wt[:, k - k0, :], a_chunk(k),
                                                                          start=(k == 0), stop=(k == DC - 1)), rd=[bw, b_f], wr=[bpt])
            kb.op(ACT, lambda e, pt=pt: e.activation(out=gst[:GW, :], in_=pt[:GW, :T], func=AF.Sigmoid), rd=[bpt], wr=[b_gst])
            kb.dma(gT[:, bass.ds(t0, T)], gst[:GW, :], rd=[b_gst], wr=[b_gT])

        kb.loop(0, c.NT, lambda i: b1_tile(i * T))
        kb.barrier()
        raw = act_sb[:, 0:S]
        hidT = act_sb[:, S:S + 2048].rearrange("p (a b) -> p a b", b=512)
        xg = [act_sb[:, S + 2048 + i * 1024: S + 2048 + (i + 1) * 1024].bitcast(F32) for i in range(3)]
        b_raw, b_hid = B("raw"), B("hid")
        b_xg = [B("xg0"), B("xg1"), B("xg2")]
        raw3 = raw.rearrange("p (c l) -> p c l", l=16)
        pos_f = tmp_sb[0]
        kb.dma(pos_f[:, 0:64].rearrange("p (a l) -> p a l", l=32), cmp_posT.rearrange("a p l -> p a l"), wr=[b_tmp[0]])
        kb.op(DVE, lambda e: e.tensor_copy(out=posT[:], in_=pos_f[:, 0:64].rearrange("p (a l) -> p a l", l=32)), rd=[b_tmp[0]], wr=[b_posT])
        kb.op(POOL, lambda e: e.memset(hidT, 0.0), wr=[b_hid])
        kb.op(POOL, lambda e: e.memset(kcmpT[:], 0.0), wr=[b_kcmp])
        for kv in range(2):
            w2t, bw2 = None, None
            for g in range(G):
                ch = (0 if kv == 0 else 12) + g
                kb.dma(raw, kvT[ch * 128:(ch + 1) * 128, :], rd=[b_kvT], wr=[b_raw])
                for hc in range(4):
                    wt, bw = load_w("cmp_w1", kv * 4096, 32, hc * 128, 128)
                    pb, bpb = next_ps()
                    for l in range(32):
                        kb.op(PE, lambda e, l=l, wt=wt, pb=pb: e.matmul(pb[:, 0:1], wt[:, l, :], posT[:, kv, l:l + 1],
                                                                       start=(l == 0), stop=(l == 31)), rd=[bw, b_posT], wr=[bpb])
                    kb.op(DVE, lambda e, pb=pb, hc=hc: e.tensor_tensor(out=cbias[:, hc:hc + 1], in0=pb[:, 0:1],
                                                                       in1=vcol(f"cmp_b1_{kv}", hc), op=ALU.add),
                          rd=[bpb, b_vecs], wr=[b_cbias])
                    pt, bpt = next_ps()
                    for l in range(32):
                        rhs = raw3[:, 0:NCMP, l] if l < 16 else raw3[:, 1:NCMP + 1, l - 16]
                        kb.op(PE, lambda e, l=l, wt=wt, pt=pt, rhs=rhs: e.matmul(pt[:, :NCMP], wt[:, l, :], rhs,
                                                                                 start=(l == 0), stop=(l == 31)), rd=[bw, b_raw], wr=[bpt])
                    x0, x1, x2 = xg
                    kb.op(ACT, lambda e, pt=pt, hc=hc: e.activation(out=x0[:, :NCMP], in_=pt[:, :NCMP], func=AF.Identity,
                                                                    bias=cbias[:, hc:hc + 1]), rd=[bpt, b_cbias], wr=[b_xg[0]])
                    kb.op(DVE, lambda e: e.tensor_tensor(out=x1[:, :NCMP], in0=x0[:, :NCMP], in1=x0[:, :NCMP], op=ALU.mult),
                          rd=[b_xg[0]], wr=[b_xg[1]])
                    kb.op(DVE, lambda e: e.tensor_scalar(out=x1[:, :NCMP], in0=x1[:, :NCMP], scalar1=0.044715, scalar2=1.0,
                                                         op0=ALU.mult, op1=ALU.add), rd=[b_xg[1]], wr=[b_xg[1]])
                    kb.op(DVE, lambda e: e.tensor_tensor(out=x1[:, :NCMP], in0=x1[:, :NCMP], in1=x0[:, :NCMP], op=ALU.mult),
                          rd=[b_xg[0], b_xg[1]], wr=[b_xg[1]])
                    kb.op(ACT, lambda e: e.activation(out=x2[:, :NCMP], in_=x1[:, :NCMP], func=AF.Sigmoid, scale=1.5957691216057308),
                          rd=[b_xg[1]], wr=[b_xg[2]])
                    kb.op(DVE, lambda e, hc=hc: e.tensor_tensor(out=hidT[:, hc, :NCMP], in0=x0[:, :NCMP], in1=x2[:, :NCMP], op=ALU.mult),
                          rd=[b_xg[0], b_xg[2]], wr=[b_hid])
                w2t, bw2 = load_w("cmp_w2", kv * 512, 4, 0, 128)
                if kv == 0:
                    pt, bpt = next_ps()
                    for hc in range(4):
                        kb.op(PE, lambda e, hc=hc, pt=pt, w2t=w2t: e.matmul(pt[:, :NCMP], w2t[:, hc, :], hidT[:, hc, :NCMP],
                                                                           start=(hc == 0), stop=(hc == 3)), rd=[bw2, b_hid], wr=[bpt])
                    kb.op(ACT, lambda e, pt=pt, g=g: e.activation(out=kcmpT[:, g, :NCMP], in_=pt[:, :NCMP], func=AF.Copy), rd=[bpt], wr=[b_kcmp])
                else:
                    for cc in range(CCH):
                        pt, bpt = next_ps()
                        for hc in range(4):
                            kb.op(PE, lambda e, hc=hc, pt=pt, w2t=w2t, cc=cc: e.matmul(pt[:, :128], hidT[:, hc, cc * 128:(cc + 1) * 128], w2t[:, hc, :],
                                                                                      start=(hc == 0), stop=(hc == 3)), rd=[bw2, b_hid], wr=[bpt])
                        kb.op(ACT, lambda e, pt=pt, g=g, cc=cc: e.activation(out=vcmp[:, g, cc, :], in_=pt[:, :128], func=AF.Copy), rd=[bpt], wr=[b_vcmp])

        kb.barrier()
        off = [0]

        def carve(n, dt=BF16):
            nb = n if dt == BF16 else 2 * n
            v = act_sb[:, off[0]:off[0] + nb]
            off[0] += (nb + 15) // 16 * 16
            assert off[0] <= ACTN, (off[0], ACTN)
            return v if dt == BF16 else v.bitcast(F32)

        woff = [0]

        def carve_w(n, dt=BF16):
            nb = n if dt == BF16 else 2 * n
            v = w_sb[1][:, woff[0]:woff[0] + nb]
            woff[0] += nb
            assert woff[0] <= WSZ, (woff[0], WSZ)
            return v if dt == BF16 else v.bitcast(F32)

        NWT = 512 // 128 + T // 128 + 1
        kselT = carve(S)
        vsel = carve(S).rearrange("p (k d) -> p k d", d=128)
        ovb = carve(CCH * (NS + 1)).rearrange("p (a n) -> p a n", n=NS + 1)
        selT = carve(T)
        e_sb = [carve(T) for _ in range(3)]
        wmask = [carve(T) for _ in range(NWT)]
        cmask = carve(T)
        tdcl = [carve(T, F32) for _ in range(2)]
        assert NS * 64 <= WSZ
        Ebig = w_sb[0][:, :NS * 64].rearrange("p (n x) -> p n x", x=64)
        kwinT = carve_w(NWT * 128)
        vwin = carve_w(NWT * 128).rearrange("p (k d) -> p k d", d=128)
        qg = carve_w(J * T).rearrange("p (j t) -> p j t", t=T)
        ocmp = carve_w(J * T, F32).rearrange("p (j t) -> p j t", t=T)
        b_ksel, b_vsel, b_kwin, b_vwin, b_qg, b_E, b_ov, b_selT, b_cmask = (B("ksel"), B("vsel"), B("kwin"), B("vwin"), B("qg"), B("E"),
                                                                          B("ov"), B("selT"), B("cmask"))
        b_e = [B("e0"), B("e1"), B("e2")]
        b_wm = [B(f"wm{i}") for i in range(NWT)]
        b_tdcl = [B("tdcl0"), B("tdcl1")]
        b_os = [B(f"os{j}") for j in range(J)]
        need_h = 3 * J * T + 3 * T + 2 * (NS + 1) + 2 * max(NS, 8) + 16 + 8 + NS + T + 64
        if need_h <= DC * TH:
            h_flat = h_sb[:].rearrange("p c t -> p (c t)")
        else:
            h_flat = sb("hx", [128, need_h], F32)[:]
        hoff = [0]

        def carve_h(n):
            v = h_flat[:, hoff[0]:hoff[0] + n]
            hoff[0] += (n + 7) // 8 * 8
            return v
        gates_sb = carve_h(3 * J * T).rearrange("p (b j t) -> p b j t", j=J, t=T)
        tm = [carve_h(T) for _ in range(3)]
        score = carve_h(2 * (NS + 1)).rearrange("p (a n) -> p a n", n=NS + 1)
        adj = carve_h(max(NS, 8))
        work = carve_h(max(NS, 8))
        m8 = carve_h(16)
        dcol = carve_h(8)
        self32 = carve_h(NS)
        rden = carve_h(T)
        b_gates, b_score, b_adj, b_work, b_m8, b_self, b_rden, b_dcol = (B("gates"), B("score"), B("adj"), B("work"), B("m8"),
                                                                        B("self"), B("rden"), B("dcol"))
        b_tm = [B("tm0"), B("tm1"), B("tm2")]
        NKT = S // 128
        assert NKT * T <= 2 * DC * T
        smask = [a_bf[:, i * T:(i + 1) * T] for i in range(NKT)]
        b_sm = [B(f"sm{i}") for i in range(NKT)]
        psc, bpsc = ps[7], b_ps[7]

        Tdk = consts[:, CL["tdk"]:CL["tdk"] + T]
        TdC = consts[:, CL["tdc"]:CL["tdc"] + T]
        ident = consts[:, CL["ident"]:CL["ident"] + 128]
        Tdk_cl = [consts[:, CL["tdk0"]:CL["tdk0"] + T], consts[:, CL["tdk1"]:CL["tdk1"] + T]]

        def td_for(dlt):
            if dlt >= 127:
                return Tdk, dlt
            assert dlt in (0, -128), dlt
            return Tdk_cl[0 if dlt == 0 else 1], 0
        kb.op(DVE, lambda e: e.tensor_copy(out=Ebig[:, :, :], in_=ident[:, 0:NS].unsqueeze(2).to_broadcast([128, NS, 64])),
              rd=[b_consts], wr=[b_E])
        kb.op(DVE, lambda e: e.tensor_copy(out=ovb[:, :, :], in_=consts[:, CL["ov"]:CL["ov"] + CCH * (NS + 1)].rearrange("p (a n) -> p a n", n=NS + 1)),
              rd=[b_consts], wr=[b_ov])
        Eflat = Ebig.rearrange("p n x -> p (n x)")

        def nxt(key, n):
            i = state.get(key, 0)
            state[key] = (i + 1) % n
            return i

        def score_tile(kT_ap, rd_k, j, hd, mask_ap, rd_mask, td, cb, acc, first, last, v_ap, rd_v, rd_td=()):
            pa, bpa = acc
            pt, bpt = next_ps()
            kb.op(PE, lambda e: e.matmul(pt[:, :T], kT_ap, qg[:, j, :], start=True, stop=True), rd=rd_k + [b_qg], wr=[bpt])
            ti = nxt("tm", 3)
            kb.op(DVE, lambda e: e.scalar_tensor_tensor(out=tm[ti], in0=td, scalar=-slopes[hd] / scale, in1=pt[:, :T],
                                                        op0=ALU.mult, op1=ALU.add), rd=[bpt, b_consts] + list(rd_td), wr=[b_tm[ti]])
            ei = nxt("e", 3)
            kb.op(ACT, lambda e: e.activation(out=e_sb[ei], in_=tm[ti], func=AF.Exp, scale=scale, bias=float(cb)),
                  rd=[b_tm[ti]], wr=[b_e[ei]])
            if mask_ap is not None:
                kb.op(POOL, lambda e: e.tensor_tensor(out=e_sb[ei], in0=e_sb[ei], in1=mask_ap, op=ALU.mult),
                      rd=[b_e[ei]] + rd_mask, wr=[b_e[ei]])
            kb.op(PE, lambda e: e.matmul(pa[:, 0:T], v_ap, e_sb[ei], start=first, stop=last, skip_group_check=True),
                  rd=rd_v + [b_e[ei]], wr=[bpa])
            kb.op(PE, lambda e: e.matmul(pa[:, T:2 * T], ones_bf[:], e_sb[ei], start=False, stop=last, skip_group_check=True),
                  rd=[b_ones, b_e[ei]], wr=[bpa])
            return ei

        def finish_branch(acc, br, j, first_branch):
            pa, bpa = acc
            kb.op(DVE, lambda e: e.tensor_scalar(out=rden, in0=pa[:, T:2 * T], scalar1=1e-30, scalar2=None, op0=ALU.max),
                  rd=[bpa], wr=[b_rden])
            kb.op(DVE, lambda e: e.reciprocal(out=rden, in_=rden), rd=[b_rden], wr=[b_rden])
            kb.op(DVE, lambda e: e.tensor_tensor(out=rden, in0=rden, in1=gates_sb[:, br, j, :], op=ALU.mult),
                  rd=[b_rden, b_gates], wr=[b_rden])
            if first_branch:
                kb.op(DVE, lambda e: e.tensor_tensor(out=ocmp[:, j, :], in0=pa[:, 0:T], in1=rden, op=ALU.mult), rd=[bpa, b_rden], wr=[b_os[j]])
            else:
                ti = nxt("tm", 3)
                kb.op(DVE, lambda e: e.tensor_tensor(out=tm[ti], in0=pa[:, 0:T], in1=rden, op=ALU.mult), rd=[bpa, b_rden], wr=[b_tm[ti]])
                kb.op(POOL, lambda e: e.tensor_tensor(out=ocmp[:, j, :], in0=ocmp[:, j, :], in1=tm[ti], op=ALU.add), rd=[b_tm[ti], b_os[j]], wr=[b_os[j]])

        for g in range(G):
            kb.dma(kselT, kvT[(4 + g) * 128:(4 + g + 1) * 128, :], rd=[b_kvT], wr=[b_ksel])
            kb.dma(vsel, vtok[0][:, g * 128:(g + 1) * 128].rearrange("(k p) d -> p k d", p=128), rd=[b_vtok], wr=[b_vsel])
            for it in range(c.NT):
                t0 = it * T
                tmax = t0 + T - 1
                kb.dma(qg, qT[g * J * 128:(g + 1) * J * 128, t0:t0 + T].rearrange("(j p) t -> p j t", p=128), rd=[b_qT], wr=[b_qg])
                for br in range(3):
                    r0 = br * H + g * J
                    kb.dma(gates_sb[:, br, :, :], gT[r0:r0 + J, t0:t0 + T].partition_broadcast(128), rd=[b_gT], wr=[b_gates])
                kt_lo = max(0, (t0 - 511) // 128)
                kt_hi = tmax // 128
                nw = kt_hi - kt_lo + 1
                kb.dma(kwinT[:, :nw * 128], kvT[(8 + g) * 128:(8 + g + 1) * 128, kt_lo * 128:(kt_hi + 1) * 128],
                       rd=[b_kvT], wr=[b_kwin])
                kb.dma(vwin[:, :nw, :], vtok[1][kt_lo * 128:(kt_hi + 1) * 128, g * 128:(g + 1) * 128].rearrange("(k p) d -> p k d", p=128),
                       rd=[b_vtok], wr=[b_vwin])
                for wi in range(nw):
                    dlt = t0 - (kt_lo + wi) * 128
                    kb.op(DVE, lambda e, wi=wi, dlt=dlt: e.tensor_scalar(out=wmask[wi], in0=Tdk, scalar1=float(dlt), scalar2=0.0,
                                                                         op0=ALU.add, op1=ALU.is_ge), rd=[b_consts], wr=[b_wm[wi]])
                    kb.op(DVE, lambda e, wi=wi, dlt=dlt: e.tensor_scalar(out=cmask, in0=Tdk, scalar1=float(dlt), scalar2=512.0,
                                                                         op0=ALU.add, op1=ALU.is_lt), rd=[b_consts], wr=[b_cmask])
                    kb.op(DVE, lambda e, wi=wi: e.tensor_tensor(out=wmask[wi], in0=wmask[wi], in1=cmask, op=ALU.mult),
                          rd=[b_cmask, b_wm[wi]], wr=[b_wm[wi]])
                cmax = (tmax - 31) // 16
                nch = 0 if cmax < 0 else cmax // 128 + 1
                kb.op(POOL, lambda e: e.memset(score, 0.0), wr=[b_score])
                cm = {}
                for cc in range(nch):
                    offc = t0 - 2048 * cc - 31
                    if offc - 16 * 127 < 0:
                        si = NKT - 1 - (len(cm) % 2)
                        ci_ = len(cm) % 2
                        mt = smask[si]
                        kb.op(DVE, lambda e, mt=mt, offc=offc: e.tensor_scalar(out=mt, in0=TdC, scalar1=float(offc), scalar2=0.0,
                                                                               op0=ALU.add, op1=ALU.is_ge), rd=[b_consts], wr=[b_sm[si]])
                        kb.op(DVE, lambda e, ci_=ci_, offc=offc: e.tensor_scalar(out=tdcl[ci_], in0=TdC, scalar1=float(offc), scalar2=0.0,
                                                                                 op0=ALU.add, op1=ALU.max), rd=[b_consts], wr=[b_tdcl[ci_]])
                        cm[cc] = (mt, b_sm[si], tdcl[ci_], b_tdcl[ci_])
                assert len(cm) <= 2
                for j in range(J):
                    hd = g * J + j
                    if nch == 0:
                        kb.op(POOL, lambda e, j=j: e.memset(ocmp[:, j, :], 0.0), wr=[b_os[j]])
                        continue
                    acc = next_acc()
                    for cc in range(nch):
                        offc = t0 - 2048 * cc - 31
                        m = cm.get(cc)
                        if m:
                            ei = score_tile(kcmpT[:, g, cc * 128:(cc + 1) * 128], [b_kcmp], j, hd, m[0], [m[1]], m[2], 0.0,
                                            acc, cc == 0, cc == nch - 1, vcmp[:, g, cc, :], [b_vcmp], rd_td=[m[3]])
                        else:
                            ei = score_tile(kcmpT[:, g, cc * 128:(cc + 1) * 128], [b_kcmp], j, hd, None, [], TdC, -slopes[hd] * offc,
                                            acc, cc == 0, cc == nch - 1, vcmp[:, g, cc, :], [b_vcmp])
                        for hf in range(T // 128):
                            kb.op(PE, lambda e, ei=ei, hf=hf, cc=cc: e.matmul(psc[:, hf * (NS + 1):(hf + 1) * (NS + 1)],
                                                                             e_sb[ei][:, hf * 128:(hf + 1) * 128], ovb[:, cc, :],
                                                                             start=(cc == 0 and hf == 0), stop=(cc == nch - 1),
                                                                             skip_group_check=True), rd=[b_e[ei], b_ov], wr=[bpsc])
                    finish_branch(acc, 0, j, True)
                    for hf in range(T // 128):
                        sc_ps = psc[:, hf * (NS + 1):(hf + 1) * (NS + 1)]
                        kb.op(DVE, lambda e, sc_ps=sc_ps: e.tensor_scalar(out=dcol[:, 0:1], in0=sc_ps[:, NS:NS + 1], scalar1=1e-30, scalar2=None,
                                                                          op0=ALU.max), rd=[bpsc], wr=[b_dcol])
                        kb.op(DVE, lambda e: e.reciprocal(out=dcol[:, 0:1], in_=dcol[:, 0:1]), rd=[b_dcol], wr=[b_dcol])
                        kb.op(DVE, lambda e, hf=hf, sc_ps=sc_ps: e.scalar_tensor_tensor(out=score[:, hf, :], in0=sc_ps, scalar=dcol[:, 0:1],
                                                                                        in1=score[:, hf, :], op0=ALU.mult, op1=ALU.add),
                              rd=[bpsc, b_dcol, b_score], wr=[b_score])
                for hf in range(T // 128):
                    cur0 = (t0 + hf * 128) // 64
                    nf = consts[:, CL["bnf"] + NS - cur0: CL["bnf"] + 2 * NS - cur0]
                    ad = consts[:, CL["badd"] + NS - cur0: CL["badd"] + 2 * NS - cur0]
                    kb.op(DVE, lambda e, hf=hf, nf=nf: e.tensor_tensor(out=adj[:, :NS], in0=score[:, hf, :NS], in1=nf, op=ALU.mult),
                          rd=[b_score, b_consts], wr=[b_adj])
                    kb.op(DVE, lambda e, ad=ad: e.tensor_tensor(out=adj[:, :NS], in0=adj[:, :NS], in1=ad, op=ALU.add), rd=[b_adj, b_consts], wr=[b_adj])
                    kb.op(DVE, lambda e: e.memset(adj[:, 0:1], 1e6), rd=[b_adj], wr=[b_adj])
                    if NS > 16:
                        kb.op(DVE, lambda e: e.max(out=m8[:, 0:8], in_=adj[:, :NS]), rd=[b_adj], wr=[b_m8])
                        kb.op(DVE, lambda e: e.match_replace(out=work[:, :NS], in_to_replace=m8[:, 0:8], in_values=adj[:, :NS], imm_value=-1e30),
                              rd=[b_adj, b_m8], wr=[b_work])
                        kb.op(DVE, lambda e: e.max(out=m8[:, 0:8], in_=work[:, :NS]), rd=[b_work], wr=[b_m8])
                        kb.op(DVE, lambda e: e.tensor_scalar(out=self32[:, :NS], in0=adj[:, :NS], scalar1=m8[:, 7:8], scalar2=None, op0=ALU.is_ge),
                              rd=[b_adj, b_m8], wr=[b_self])
                    else:
                        kb.op(DVE, lambda e: e.memset(self32[:, :NS], 1.0), wr=[b_self])
                    ptr, bptr = next_ps()
                    kb.op(PE, lambda e, ptr=ptr: e.transpose(ptr[:NS, :128], self32[:, :NS], ident), rd=[b_self, b_consts], wr=[bptr])
                    kb.op(ACT, lambda e, ptr=ptr, hf=hf: e.activation(out=selT[:NS, hf * 128:(hf + 1) * 128], in_=ptr[:NS, :128], func=AF.Copy),
                          rd=[bptr], wr=[b_selT])
                nkt = tmax // 128 + 1
                for kt in range(nkt):
                    pm, bpm = next_ps()
                    kb.op(PE, lambda e, kt=kt, pm=pm: e.matmul(pm[:, :T], Eflat[:NS, kt * 128:(kt + 1) * 128], selT[:NS, :], start=True, stop=True),
                          rd=[b_E, b_selT], wr=[bpm])
                    kb.op(ACT, lambda e, kt=kt, pm=pm: e.activation(out=smask[kt], in_=pm[:, :T], func=AF.Copy), rd=[bpm], wr=[b_sm[kt]])
                    if kt * 128 + 127 > t0:
                        dlt = t0 - kt * 128
                        kb.op(DVE, lambda e, dlt=dlt: e.tensor_scalar(out=cmask, in0=Tdk, scalar1=float(dlt), scalar2=0.0,
                                                                      op0=ALU.add, op1=ALU.is_ge), rd=[b_consts], wr=[b_cmask])
                        kb.op(DVE, lambda e, kt=kt: e.tensor_tensor(out=smask[kt], in0=smask[kt], in1=cmask, op=ALU.mult),
                              rd=[b_cmask, b_sm[kt]], wr=[b_sm[kt]])
                for j in range(J):
                    hd = g * J + j
                    acc = next_acc()
                    for kt in range(nkt):
                        td, cbm = td_for(t0 - kt * 128)
                        score_tile(kselT[:, kt * 128:(kt + 1) * 128], [b_ksel], j, hd, smask[kt], [b_sm[kt]], td,
                                   -slopes[hd] * cbm, acc, kt == 0, kt == nkt - 1, vsel[:, kt, :], [b_vsel])
                    finish_branch(acc, 1, j, False)
                    acc = next_acc()
                    for wi in range(nw):
                        kt = kt_lo + wi
                        td, cbm = td_for(t0 - kt * 128)
                        score_tile(kwinT[:, wi * 128:(wi + 1) * 128], [b_kwin], j, hd, wmask[wi], [b_wm[wi]], td,
                                   -slopes[hd] * cbm, acc, wi == 0, wi == nw - 1, vwin[:, wi, :], [b_vwin])
                    finish_branch(acc, 2, j, False)
                    st, bst = stg()
                    kb.op(ACT, lambda e, st=st, j=j: e.activation(out=st[:, :T], in_=ocmp[:, j, :], func=AF.Copy), rd=[b_os[j]], wr=[bst])
                    kb.dma(oT[hd * 128:(hd + 1) * 128, t0:t0 + T], st[:, :T], rd=[bst], wr=[b_oT])

        kb.barrier()
        ov_ = oT.rearrange("(c p) s -> p c s", p=128)
        outv = outT.rearrange("(c p) s -> p c s", p=128)
        o_in = act_sb[:, :DC * T].rearrange("p (c t) -> p c t", t=T)
        b_outd = B("outdummy")

        def b4_tile(t0, first):
            kb.dma(h_sb[:, :, HALO:], hv[:, :, bass.ds(t0, T)], rd=[b_hT], wr=[b_h])
            kb.dma(o_in, ov_[:, :, bass.ds(t0, T)], rd=[b_oT], wr=[b_act])

            def ev_o(i, pt, bpt):
                kb.op(ACT, lambda e: e.activation(out=f_sb[:, i, :], in_=pt[:, :T], func=AF.Copy), rd=[bpt], wr=[b_f])
            proj("nsa_w_out", lambda k: o_in[:, k, :], [b_act], DC, [m * 128 for m in range(DC)], ev_o)
            residual_add("ln_mix1_1")
            xa_block(1)
            ffn_block(1, first)
            kb.dma(outv[:, :, bass.ds(t0, T)], h_sb[:, :, HALO:], rd=[b_h], wr=[b_outd])
        b4_tile(0, True)
        kb.loop(1, c.NT, lambda i: b4_tile(i * T, False))

    layer0()
    if depth > 1:
        nsa_layer()
    kb.finish()
    return nc, es


def vec_layout(c):
    L = {}
    n = 0

    def add(name, w):
        nonlocal n
        L[name] = n
        n += w
    for l in range(c.depth):
        for nm in ("ln_mix", "ln_xa", "ln_ffn"):
            for i in range(2):
                add(f"{nm}{l}_{i}", c.DC)
        for r in range(3):
            add(f"conv_w{l}_{r}", c.FC)
        add(f"conv_b{l}", c.FC)
    add("mem_norm", c.DC)
    add("pool_scale", c.DC)
    add("cmp_b1_0", 4)
    add("cmp_b1_1", 4)
    L["_n"] = n
    return L


def const_layout(c):
    L = {}
    n = 0

    def add(name, w):
        nonlocal n
        L[name] = n
        n += w
    add("ones", 128)
    add("ident", 128)
    add("invc", 64)
    add("tdk", c.T)
    add("tdc", c.T)
    add("tdk0", c.T)
    add("tdk1", c.T)
    add("ov", ((c.NCMP + 127) // 128) * (c.NS + 1))
    add("bnf", 2 * c.NS)
    add("badd", 2 * c.NS)
    L["_n"] = n
    return L


def fm(v):
    v = np.asarray(v, np.float32)
    return np.ascontiguousarray(v.reshape(-1, 128).T)


def make_inputs(c, b, inp):
    VL = vec_layout(c)
    vecs = np.zeros((128, VL["_n"]), np.float32)

    def put(name, v):
        a = fm(v)
        vecs[:, VL[name]:VL[name] + a.shape[1]] = a
    for l in range(c.depth):
        for nm in ("ln_mix", "ln_xa", "ln_ffn"):
            for i in range(2):
                put(f"{nm}{l}_{i}", inp[nm][l, i])
        for r in range(3):
            put(f"conv_w{l}_{r}", inp["ffn_conv_w"][l, r])
        put(f"conv_b{l}", inp["ffn_conv_b"][l])
    put("mem_norm", inp["mem_norm"])
    put("pool_scale", inp["pool_scale"][0])
    if c.depth > 1:
        put("cmp_b1_0", inp["nsa_cmp_b1"][0, 0])
        put("cmp_b1_1", inp["nsa_cmp_b1"][0, 1])
    CL = const_layout(c)
    consts = np.zeros((128, CL["_n"]), np.float32)
    consts[:, CL["ones"]:CL["ones"] + 128] = 1.0
    consts[:, CL["ident"]:CL["ident"] + 128] = np.eye(128, dtype=np.float32)
    for g, win in enumerate(POOL_WINDOWS):
        consts[:, CL["invc"] + g * 16: CL["invc"] + (g + 1) * 16] = 1.0 / np.minimum(np.arange(16) + 1, win)
    T = c.T
    ql = np.arange(T)[None, :].astype(np.float32)
    kl = np.arange(128)[:, None].astype(np.float32)
    consts[:, CL["tdk"]:CL["tdk"] + T] = ql - kl
    consts[:, CL["tdc"]:CL["tdc"] + T] = ql - 16 * kl
    consts[:, CL["tdk0"]:CL["tdk0"] + T] = np.maximum(ql - kl, 0)
    consts[:, CL["tdk1"]:CL["tdk1"] + T] = np.maximum(ql - kl - 128, 0)
    CCH = (c.NCMP + 127) // 128
    NS = c.NS
    ov = np.zeros((128, CCH, NS + 1), np.float32)
    for cc in range(CCH):
        cidx = cc * 128 + np.arange(128)
        n = np.arange(NS)
        o = ((cidx[:, None] >= 4 * n[None, :] - 1) & (cidx[:, None] <= 4 * n[None, :] + 3) & (cidx[:, None] < c.NCMP))
        ov[:, cc, :NS] = o
        ov[:, cc, NS] = 1.0
    consts[:, CL["ov"]:CL["ov"] + CCH * (NS + 1)] = ov.reshape(128, -1)
    qb = (np.arange(128) // 64)[:, None]
    xx = np.arange(2 * NS)[None, :] - NS
    nf = (xx <= qb).astype(np.float32)
    ff = ((xx == qb) | (xx == qb - 1)).astype(np.float32)
    consts[:, CL["bnf"]:CL["bnf"] + 2 * NS] = nf
    consts[:, CL["badd"]:CL["badd"] + 2 * NS] = nf - 1.0 + 1e6 * ff
    m = {
        "xT": np.ascontiguousarray(np.asarray(inp["x"][b], np.float32).T),
        "memT": np.ascontiguousarray(np.asarray(inp["mem"][b], np.float32).T),
        "vecs": vecs,
        "consts": consts,
        "pool_w": np.ascontiguousarray(np.asarray(inp["pool_w"][0], np.float32).reshape(4 * c.GC, c.GC)),
    }
    for l in range(c.depth):
        m[f"xa_wq{l}"] = np.asarray(inp["xa_wq"][l], np.float32)
        m[f"xa_wkv{l}"] = np.asarray(inp["xa_wkv"][l], np.float32)
        m[f"xa_wo{l}"] = np.asarray(inp["xa_wo"][l], np.float32)
        m[f"w_gu{l}"] = np.asarray(inp["ffn_w_gu"][l], np.float32)
        m[f"w_down{l}"] = np.asarray(inp["ffn_w_down"][l], np.float32)
    if c.depth > 1:
        m["nsa_w_in"] = np.asarray(inp["nsa_w_in"][0], np.float32)
        m["nsa_w_out"] = np.asarray(inp["nsa_w_out"][0], np.float32)
        m["cmp_posT"] = np.ascontiguousarray(np.asarray(inp["nsa_cmp_pos"][0], np.float32).transpose(0, 2, 1))
        m["cmp_w1"] = np.ascontiguousarray(np.asarray(inp["nsa_cmp_w1"][0], np.float32).reshape(2 * 4096, 512))
        m["cmp_w2"] = np.ascontiguousarray(np.asarray(inp["nsa_cmp_w2"][0], np.float32).reshape(2 * 512, 128))
    return m


def run(c, inp, n_layers=None):
    nc, es = build(c, n_layers=n_layers)
    Bn = inp["x"].shape[0]
    in_maps = [make_inputs(c, b, inp) for b in range(Bn)]
    with es:
        pass
    res = run_bass_kernel_spmd(nc, in_maps, core_ids=list(range(Bn)))
    out = np.stack([np.ascontiguousarray(res.results[b]["outT"].T) for b in range(Bn)])
    return out


def kernel(**inputs):
    c = Cfg()
    return run(c, inputs).astype(np.float32)
```

```python
import contextlib
import numpy as np
import concourse.bass as bass
import concourse.mybir as mybir
from concourse.bass_utils import run_bass_kernel_spmd

F32 = mybir.dt.float32
BF16 = mybir.dt.bfloat16
AF = mybir.ActivationFunctionType
ALU = mybir.AluOpType
RMS_EPS = 1e-6
POOL_WINDOWS = (2, 4, 8, 16)


class Cfg:
    def __init__(s, D=4096, S=8192, DFF=11008, T=256, depth=2, M=256):
        s.D, s.S, s.DFF, s.T, s.depth, s.M = D, S, DFF, T, depth, M
        s.DC = D // 128
        s.FC = DFF // 128
        s.NT = S // T
        s.GC = D // 4
        s.GCC = s.GC // 128
        s.H = D // 128
        s.G = 4
        s.J = s.H // 4
        s.QW = D
        s.KVW = 3 * 2 * 4 * 128
        s.GW = 3 * s.H
        s.INW = s.QW + s.KVW + s.GW
        s.NCMP = S // 16 - 1
        s.NS = S // 64
        s.HALO = 16


class Eng:
    def __init__(s, name, h, sem):
        s.name, s.h, s.sem, s.cnt, s.known = name, h, sem, 0, {}
        s.dsems = []
        s.dcnt = 0


class Buf:
    __slots__ = ("w", "r", "name")
    ALL = []

    def __init__(s, name=""):
        s.w = None
        s.r = {}
        s.name = name
        Buf.ALL.append(s)


class KB:
    def __init__(s, nc, es):
        s.nc = nc
        s.es = es
        mk = lambda n: es.enter_context(nc.semaphore(n))
        s.PE = Eng("pe", nc.tensor, mk("s_pe"))
        s.ACT = Eng("act", nc.scalar, mk("s_act"))
        s.DVE = Eng("dve", nc.vector, mk("s_dve"))
        s.POOL = Eng("pool", nc.gpsimd, mk("s_pool"))
        s.SP = Eng("sp", nc.sync, mk("s_sp"))
        s.SP.dsems = [mk(f"d_sp{i}") for i in range(8)]
        s.POOL.dsems = [mk(f"d_pl{i}") for i in range(8)]
        s.ACT.dsems = [mk(f"d_ac{i}") for i in range(4)]
        s.BA = mk("bar_a")
        s.BB = mk("bar_b")
        s.ep = 0
        Buf.ALL.clear()

    def _sync(s, E, rd, wr):
        toks = []
        for b in rd:
            if b.w is not None:
                toks.append(b.w)
        for b in wr:
            if b.w is not None:
                toks.append(b.w)
            toks.extend(b.r.values())
        for (sem, val, src) in toks:
            if E is s.PE and src is s.PE:
                continue
            k = id(sem)
            if E.known.get(k, 0) >= val:
                continue
            E.h.wait_ge(sem, val)
            E.known[k] = val

    def _mark(s, tok, rd, wr):
        k = id(tok[0])
        for b in rd:
            b.r[k] = tok
        for b in wr:
            b.w = tok
            b.r = {}

    def op(s, E, fn, rd=(), wr=()):
        s._sync(E, rd, wr)
        ins = fn(E.h)
        E.cnt += 1
        ins.then_inc(E.sem, 1)
        s._mark((E.sem, E.cnt, E), rd, wr)

    def dma(s, out, in_, rd=(), wr=(), q=None, **kw):
        Q = q or s.SP
        K = len(Q.dsems)
        i = Q.dcnt
        sem = Q.dsems[i % K]
        if i >= K:
            v = 16 * (i // K)
            if Q.known.get(id(sem), 0) < v:
                Q.h.wait_ge(sem, v)
                Q.known[id(sem)] = v
        s._sync(Q, rd, wr)
        Q.h.dma_start(out=out, in_=in_, **kw).then_inc(sem, 16)
        Q.dcnt += 1
        tok = (sem, 16 * (i // K + 1), None)
        s._mark(tok, rd, wr)
        return tok

    def epoch(s, ep_val=None):
        static = ep_val is None
        if static:
            ep_val = s.ep
            s.ep += 1
        engs = (s.PE, s.ACT, s.DVE, s.POOL, s.SP)
        for E in engs:
            if E.cnt:
                E.h.wait_ge(E.sem, E.cnt)
            K = len(E.dsems)
            for j in range(min(K, E.dcnt)):
                E.h.wait_ge(E.dsems[j], 16 * ((E.dcnt - 1 - j) // K + 1))
            E.h.sem_inc(s.BA, 1)
        for E in engs:
            E.h.wait_ge(s.BA, 5 * (ep_val + 1))
            E.h.sem_clear(E.sem)
            K = len(E.dsems)
            for j in range(min(K, E.dcnt)):
                E.h.sem_clear(E.dsems[j])
            E.h.sem_inc(s.BB, 1)
        for E in engs:
            E.h.wait_ge(s.BB, 5 * (ep_val + 1))
            E.cnt = 0
            E.known = {}
            E.dcnt = 0
        for b in Buf.ALL:
            b.w = None
            b.r = {}

    def barrier(s):
        s.epoch()

    def loop(s, n0, n1, body):
        if n1 <= n0:
            return
        s.epoch()
        ep0 = s.ep
        with s.nc.Fori(n0, n1) as i:
            body(i)
            s.epoch(ep0 + (i - n0))
        s.ep = ep0 + (n1 - n0)

    def finish(s):
        for Q in (s.SP, s.POOL, s.ACT):
            K = len(Q.dsems)
            for j in range(min(K, Q.dcnt)):
                n = (Q.dcnt - 1 - j) // K + 1
                Q.h.wait_ge(Q.dsems[j], 16 * n)


def build(cfg, n_layers=None, debug_out=None):
    c = cfg
    D, S, DFF, T, DC, FC, M = c.D, c.S, c.DFF, c.T, c.DC, c.FC, c.M
    HALO = c.HALO
    TH = T + HALO
    nc = bass.Bass("TRN2", target_bir_lowering=False)
    es = contextlib.ExitStack()
    kb = KB(nc, es)
    PE, ACT, DVE, POOL, SP = kb.PE, kb.ACT, kb.DVE, kb.POOL, kb.SP
    depth = c.depth if n_layers is None else n_layers

    def din(name, shape, dt=F32):
        return nc.dram_tensor(name, list(shape), dt, kind="ExternalInput").ap()

    def dscr(name, shape, dt):
        return nc.dram_tensor(name, list(shape), dt, kind="Internal").ap()

    xT = din("xT", [D, S])
    memT = din("memT", [D, M])
    NV = vec_layout(c)["_n"]
    VL = vec_layout(c)
    vecs_d = din("vecs", [128, NV])
    consts_d = din("consts", [128, const_layout(c)["_n"]])
    CL = const_layout(c)
    outT = nc.dram_tensor("outT", [D, S], F32, kind="ExternalOutput").ap()
    hT = dscr("hT", [D, S], F32)

    wf = {}
    wb = {}
    wshapes = {}
    wshapes["pool_w"] = [4 * c.GC, c.GC]
    for l in range(c.depth):
        wshapes[f"xa_wq{l}"] = [D, 512]
        wshapes[f"xa_wkv{l}"] = [D, 1024]
        wshapes[f"xa_wo{l}"] = [512, D]
        wshapes[f"w_gu{l}"] = [D, 2 * DFF]
        wshapes[f"w_down{l}"] = [DFF, D]
    if c.depth > 1:
        wshapes["nsa_w_in"] = [D, c.INW]
        wshapes["nsa_w_out"] = [D, D]
        wshapes["cmp_w1"] = [2 * 4096, 512]
        wshapes["cmp_w2"] = [2 * 512, 128]
    if c.depth > 1:
        cmp_posT = din("cmp_posT", [2, 128, 32])
    for k, shp in wshapes.items():
        wf[k] = din(k, shp)
        wb[k] = dscr(k + "_b", shp, BF16)

    def sb(name, shape, dt):
        return es.enter_context(nc.sbuf_tensor(name, list(shape), dt))

    h_sb = sb("h_sb", [128, DC, TH], F32)
    f_sb = sb("f_sb", [128, DC, T], F32)
    _CCH = (c.NCMP + 127) // 128
    _NWT = 512 // 128 + T // 128 + 1
    ACTN = max(FC * T, 3 * c.GCC * TH * 2 + c.GCC * T, DC * M * 3, DC * T)
    if c.depth > 1:
        ACTN = max(ACTN, c.H * T + 16 * T + 2 * (T // 128) * 512, S + 2048 + 3072, 2 * S + _CCH * (c.NS + 1) + 64 + T * (1 + 3 + _NWT + 1 + 4))
    ACTN = (ACTN + 63) // 64 * 64
    act_sb = sb("act_sb", [128, ACTN], BF16)
    NWS = 2
    WSZ = 8192
    w_sb = [sb(f"w_sb{i}", [128, WSZ], BF16) for i in range(NWS)]
    vecs = sb("vecs_sb", [128, NV], F32)
    consts = sb("consts_sb", [128, CL["_n"]], F32)
    sq_sb = [sb(f"sq{i}", [128, TH], F32) for i in range(2)]
    rstd_sb = sb("rstd", [128, TH], F32)
    tmp_sb = [sb(f"tmp{i}", [128, TH], F32) for i in range(4)]
    graw_sb = [sb(f"graw{i}", [128, T + 2], F32) for i in range(2)]
    carry_sb = sb("carry", [128, FC, 2], F32)
    sil_sb = [sb(f"sil{i}", [128, T], F32) for i in range(2)]
    ones_bf = sb("ones_bf", [128, 128], BF16)
    q_sb = sb("q_sb", [128, 4, T], BF16)
    kT_sb = sb("kT_sb", [128, 4, M], BF16)
    v_sb = sb("v_sb", [128, M // 128, 512], BF16)
    p_sb = [sb(f"p_sb{i}", [128, T], BF16) for i in range(2)]
    o_sb = sb("o_sb", [128, 4, T], BF16)

    ps = [es.enter_context(nc.psum_tensor(f"ps{i}", [128, 512], F32)) for i in range(8)]

    B = lambda n: Buf(n)
    b_h, b_f, b_act, b_vecs, b_consts = B("h"), B("f"), B("act"), B("vecs"), B("consts")
    b_w = [B(f"w{i}") for i in range(NWS)]
    b_sq = [B("sq0"), B("sq1")]
    b_rstd = B("rstd")
    b_tmp = [B(f"tmp{i}") for i in range(4)]
    b_graw = [B("graw0"), B("graw1")]
    b_carry = B("carry")
    b_sil = [B("sil0"), B("sil1")]
    b_ones = B("ones")
    b_q, b_kT, b_v, b_o = B("q"), B("kT"), B("v"), B("o")
    b_p = [B("p0"), B("p1")]
    b_ps = [B(f"ps{i}") for i in range(8)]
    b_hT = B("hT")
    b_wb = {k: B("wb_" + k) for k in wb}

    ones_f = consts[:, CL["ones"]:CL["ones"] + 128]

    state = {"ps": 0, "w": 0, "sq": 0, "tmp": 0, "p": 0, "acc": 0, "e": 0, "st": 0}

    def next_ps():
        i = state["ps"]
        state["ps"] = (i + 1) % 5
        return ps[2 + i], b_ps[2 + i]

    def next_acc():
        i = state["acc"]
        state["acc"] = i ^ 1
        return ps[i], b_ps[i]

    def next_w():
        i = state["w"]
        state["w"] = (i + 1) % NWS
        return w_sb[i], b_w[i]

    kb.dma(vecs[:], vecs_d[:, :], wr=[b_vecs])
    kb.dma(consts[:], consts_d[:, :], wr=[b_consts])
    kb.op(DVE, lambda e: e.memset(ones_bf[:], 1.0), wr=[b_ones])

    def cast_weight(k):
        src, dst = wf[k], wb[k]
        R, Ccols = wshapes[k]
        tot = R * Ccols
        W = 2048
        while tot % W:
            W //= 2
        rows = tot // W
        s2 = src.rearrange("r c -> (r c)").rearrange("(a b) -> a b", b=W)
        d2 = dst.rearrange("r c -> (r c)").rearrange("(a b) -> a b", b=W)
        step = 2048
        for r0 in range(0, rows, step):
            r1 = min(rows, r0 + step)
            kb.dma(d2[r0:r1, :], s2[r0:r1, :], wr=[b_wb[k]], q=POOL)

    TMK = {}
    wt_m = {}
    b_wt = {}
    for k, (R_, C_) in wshapes.items():
        if k.startswith(("w_gu", "w_down", "xa_wq", "xa_wo")) or k in ("nsa_w_out", "pool_w", "nsa_w_in"):
            ncols_tm = C_ if k != "nsa_w_in" else (c.QW + c.KVW)
            if ncols_tm % 256 == 0 and R_ % 128 == 0:
                TMK[k] = (ncols_tm // 256, R_ // 128)
                wt_m[k] = dscr(k + "_t", [ncols_tm // 256, 128, R_ // 128, 256], BF16)
                b_wt[k] = B("wt_" + k)

    def rearr_weight(k):
        NB_, KC_ = TMK[k]
        for blk in range(NB_):
            kb.dma(wt_m[k][blk], wb[k][:, blk * 256:(blk + 1) * 256].rearrange("(kc p) n -> p kc n", p=128),
                   rd=[b_wb[k]], wr=[b_wt[k]])

    cast_order = ["pool_w"]
    for l in range(depth):
        cast_order += [f"xa_wkv{l}", f"xa_wq{l}", f"xa_wo{l}", f"w_gu{l}", f"w_down{l}"]
        if l == 0 and depth > 1:
            cast_order += ["nsa_w_in", "cmp_w1", "cmp_w2", "nsa_w_out"]
    for k in cast_order:
        if k in wb:
            cast_weight(k)
    for k in cast_order:
        if k in TMK:
            rearr_weight(k)

    def vcol(name, idx=0, n=1):
        o = VL[name] + idx
        return vecs[:, o:o + n]

    def rms_stats(chunks, ncols, rd):
        pt, bpt = next_ps()
        n = len(chunks)
        for i, ap in enumerate(chunks):
            j = state["sq"]
            state["sq"] ^= 1
            kb.op(ACT, lambda e, ap=ap, j=j: e.activation(out=sq_sb[j][:, :ncols], in_=ap, func=AF.Square),
                  rd=rd, wr=[b_sq[j]])
            kb.op(PE, lambda e, i=i, j=j: e.matmul(pt[:, :ncols], ones_f, sq_sb[j][:, :ncols],
                                                   start=(i == 0), stop=(i == n - 1)),
                  rd=[b_sq[j], b_consts], wr=[bpt])
        kb.op(DVE, lambda e: e.tensor_scalar(out=rstd_sb[:, :ncols], in0=pt[:, :ncols], scalar1=1.0 / D,
                                             scalar2=RMS_EPS, op0=ALU.mult, op1=ALU.add),
              rd=[bpt], wr=[b_rstd])
        kb.op(ACT, lambda e: e.activation(out=rstd_sb[:, :ncols], in_=rstd_sb[:, :ncols], func=AF.Sqrt),
              rd=[b_rstd], wr=[b_rstd])
        kb.op(DVE, lambda e: e.reciprocal(out=rstd_sb[:, :ncols], in_=rstd_sb[:, :ncols]),
              rd=[b_rstd], wr=[b_rstd])

    def a_chunk(cidx):
        return a_bf[:, cidx * T:(cidx + 1) * T]

    a_bf = f_sb[:].rearrange("p c t -> p (c t)").bitcast(BF16)

    def norm_to_a(gname, src_fn, rd_src):
        rms_stats([src_fn(ci) for ci in range(DC)], T, rd_src)
        for ci in range(DC):
            kb.op(DVE, lambda e, ci=ci: e.scalar_tensor_tensor(out=a_chunk(ci), in0=src_fn(ci),
                                                                scalar=vcol(gname, ci), in1=rstd_sb[:, :T],
                                                                op0=ALU.mult, op1=ALU.mult),
                  rd=rd_src + [b_rstd, b_vecs], wr=[b_f])

    def h_c(ci):
        return h_sb[:, ci, HALO:HALO + T]

    def residual_add(gname):
        rms_stats([f_sb[:, ci, :] for ci in range(DC)], T, [b_f])
        for ci in range(DC):
            k = state["tmp"]
            state["tmp"] = (k + 1) % 4
            kb.op(DVE, lambda e, ci=ci, k=k: e.scalar_tensor_tensor(out=tmp_sb[k][:, :T], in0=f_sb[:, ci, :],
                                                                     scalar=vcol(gname, ci), in1=rstd_sb[:, :T],
                                                                     op0=ALU.mult, op1=ALU.mult),
                  rd=[b_f, b_rstd, b_vecs], wr=[b_tmp[k]])
            kb.op(POOL, lambda e, ci=ci, k=k: e.tensor_tensor(out=h_c(ci), in0=h_c(ci), in1=tmp_sb[k][:, :T],
                                                              op=ALU.add),
                  rd=[b_tmp[k], b_h], wr=[b_h])

    def load_w(key, r0, nr, c0, ncol):
        wt, bw = next_w()
        assert nr * ncol <= WSZ
        dst = wt[:, :nr * ncol].rearrange("p (k n) -> p k n", n=ncol)
        if key in TMK and ncol == 256 and c0 % 256 == 0 and r0 % 128 == 0 and c0 // 256 < TMK[key][0]:
            src = wt_m[key][c0 // 256][:, r0 // 128:r0 // 128 + nr, :]
            kb.dma(dst, src, rd=[b_wt[key]], wr=[bw])
        else:
            src = wb[key][r0:r0 + nr * 128, c0:c0 + ncol].rearrange("(k p) n -> p k n", p=128)
            kb.dma(dst, src, rd=[b_wb[key]], wr=[bw])
        return dst, bw

    def proj(key, in_fn, rd_in, KC, cols, evac, r0=0, nb=256):
        i = 0
        while i < len(cols):
            grp = [cols[i]]
            while len(grp) * 128 < nb and i + len(grp) < len(cols) and cols[i + len(grp)] == grp[-1] + 128:
                grp.append(cols[i + len(grp)])
            ncol = len(grp) * 128
            kstep = max(1, WSZ // ncol)
            pts = [next_ps() for _ in grp]
            for k0 in range(0, KC, kstep):
                k1 = min(KC, k0 + kstep)
                wt, bw = load_w(key, r0 + k0 * 128, k1 - k0, grp[0], ncol)
                for gi in range(len(grp)):
                    pt, bpt = pts[gi]
                    for k in range(k0, k1):
                        kb.op(PE, lambda e, pt=pt, wt=wt, k=k, k0=k0, gi=gi: e.matmul(
                            pt[:, :T], wt[:, k - k0, gi * 128:(gi + 1) * 128], in_fn(k),
                            start=(k == 0), stop=(k == KC - 1)),
                              rd=[bw] + rd_in, wr=[bpt])
            for gi in range(len(grp)):
                evac(i + gi, pts[gi][0], pts[gi][1])
            i += len(grp)

    def xa_prepare(l):
        mem_f = act_sb[:, :DC * M * 2].bitcast(F32).rearrange("p (c m) -> p c m", m=M)
        memn = act_sb[:, DC * M * 2: DC * M * 3].rearrange("p (c m) -> p c m", m=M)
        kb.dma(mem_f, memT.rearrange("(c p) m -> p c m", p=128), wr=[b_act])
        rms_stats([mem_f[:, ci, :] for ci in range(DC)], M, [b_act])
        for ci in range(DC):
            kb.op(DVE, lambda e, ci=ci: e.scalar_tensor_tensor(out=memn[:, ci, :], in0=mem_f[:, ci, :],
                                                                scalar=vcol("mem_norm", ci), in1=rstd_sb[:, :M],
                                                                op0=ALU.mult, op1=ALU.mult),
                  rd=[b_act, b_rstd, b_vecs], wr=[b_act])
        key = f"xa_wkv{l}"
        for hh in range(4):
            pt, bpt = next_ps()
            kstep = WSZ // 128
            for k0 in range(0, DC, kstep):
                k1 = min(DC, k0 + kstep)
                wt, bw = load_w(key, k0 * 128, k1 - k0, hh * 128, 128)
                for k in range(k0, k1):
                    kb.op(PE, lambda e, pt=pt, wt=wt, k=k, k0=k0: e.matmul(pt[:, :M], wt[:, k - k0, :], memn[:, k, :],
                                                                          start=(k == 0), stop=(k == DC - 1)),
                          rd=[bw, b_act], wr=[bpt])
            kb.op(ACT, lambda e, pt=pt, hh=hh: e.activation(out=kT_sb[:, hh, :], in_=pt[:, :M], func=AF.Copy),
                  rd=[bpt], wr=[b_kT])
        for mc in range(M // 128):
            pt, bpt = next_ps()
            kstep = WSZ // 512
            for k0 in range(0, DC, kstep):
                k1 = min(DC, k0 + kstep)
                wt, bw = load_w(key, k0 * 128, k1 - k0, 512, 512)
                for k in range(k0, k1):
                    kb.op(PE, lambda e, pt=pt, wt=wt, k=k, k0=k0, mc=mc: e.matmul(
                        pt[:, :512], memn[:, k, mc * 128:(mc + 1) * 128], wt[:, k - k0, :],
                        start=(k == 0), stop=(k == DC - 1)),
                          rd=[bw, b_act], wr=[bpt])
            kb.op(ACT, lambda e, pt=pt, mc=mc: e.activation(out=v_sb[:, mc, :], in_=pt[:, :512], func=AF.Copy),
                  rd=[bpt], wr=[b_v])

    def xa_block(l):
        norm_to_a(f"ln_xa{l}_0", h_c, [b_h])
        def evq(i, pt, bpt):
            kb.op(ACT, lambda e: e.activation(out=q_sb[:, i, :], in_=pt[:, :T], func=AF.Copy), rd=[bpt], wr=[b_q])
        proj(f"xa_wq{l}", lambda k: a_chunk(k), [b_f], DC, [hh * 128 for hh in range(4)], evq)
        scale = 128.0 ** -0.5
        for hh in range(4):
            po, bpo = next_ps()
            pss = []
            for mc in range(M // 128):
                pt, bpt = next_ps()
                kb.op(PE, lambda e, pt=pt, mc=mc: e.matmul(pt[:, :T], kT_sb[:, hh, mc * 128:(mc + 1) * 128], q_sb[:, hh, :],
                                                          start=True, stop=True), rd=[b_kT, b_q], wr=[bpt])
                j = state["p"]
                state["p"] ^= 1
                kb.op(ACT, lambda e, pt=pt, j=j: e.activation(out=p_sb[j][:], in_=pt[:, :T], func=AF.Exp, scale=scale),
                      rd=[bpt], wr=[b_p[j]])
                pss.append(j)
            pd, bpd = next_ps()
            for mc, j in enumerate(pss):
                kb.op(PE, lambda e, mc=mc, j=j: e.matmul(po[:, :T], v_sb[:, mc, hh * 128:(hh + 1) * 128], p_sb[j][:],
                                                        start=(mc == 0), stop=(mc == M // 128 - 1)),
                      rd=[b_v, b_p[j]], wr=[bpo])
            for mc, j in enumerate(pss):
                kb.op(PE, lambda e, mc=mc, j=j: e.matmul(pd[:, :T], ones_bf[:], p_sb[j][:],
                                                        start=(mc == 0), stop=(mc == M // 128 - 1)),
                      rd=[b_ones, b_p[j]], wr=[bpd])
            k = state["tmp"]
            state["tmp"] = (k + 1) % 4
            kb.op(DVE, lambda e, k=k: e.reciprocal(out=tmp_sb[k][:, :T], in_=pd[:, :T]), rd=[bpd], wr=[b_tmp[k]])
            kb.op(DVE, lambda e, k=k, hh=hh: e.tensor_tensor(out=o_sb[:, hh, :], in0=po[:, :T], in1=tmp_sb[k][:, :T],
                                                             op=ALU.mult), rd=[bpo, b_tmp[k]], wr=[b_o])
        def evo(i, pt, bpt):
            kb.op(ACT, lambda e: e.activation(out=f_sb[:, i, :], in_=pt[:, :T], func=AF.Copy), rd=[bpt], wr=[b_f])
        proj(f"xa_wo{l}", lambda k: o_sb[:, k, :], [b_o], 4, [m * 128 for m in range(DC)], evo)
        residual_add(f"ln_xa{l}_1")

    def ffn_block(l, first_tile):
        norm_to_a(f"ln_ffn{l}_0", h_c, [b_h])
        if first_tile:
            kb.op(POOL, lambda e: e.memset(carry_sb[:], 0.0), wr=[b_carry])
        act3 = act_sb[:, :FC * T].rearrange("p (j t) -> p j t", t=T)
        key = f"w_gu{l}"
        JB = 2
        for j0 in range(0, FC, JB):
            js = list(range(j0, min(FC, j0 + JB)))
            sil_of = {}

            def ev_gate(i, pt, bpt, js=js, sil_of=sil_of):
                j = js[i]
                g = j % 2
                gr = graw_sb[g]
                kb.op(POOL, lambda e: e.tensor_copy(out=gr[:, 0:2], in_=carry_sb[:, j, :]), rd=[b_carry], wr=[b_graw[g]])
                kb.op(ACT, lambda e: e.activation(out=gr[:, 2:2 + T], in_=pt[:, :T], func=AF.Copy), rd=[bpt], wr=[b_graw[g]])
                kb.op(POOL, lambda e: e.tensor_copy(out=carry_sb[:, j, :], in_=gr[:, T:T + 2]), rd=[b_graw[g]], wr=[b_carry])
                k = state["tmp"]
                state["tmp"] = (k + 1) % 4
                tt = tmp_sb[k]
                cw = lambda r: vcol(f"conv_w{l}_{r}", j)
                kb.op(DVE, lambda e: e.tensor_scalar(out=tt[:, :T], in0=gr[:, 2:2 + T], scalar1=cw(2),
                                                     scalar2=vcol(f"conv_b{l}", j), op0=ALU.mult, op1=ALU.add),
                      rd=[b_graw[g], b_vecs], wr=[b_tmp[k]])
                kb.op(DVE, lambda e: e.scalar_tensor_tensor(out=tt[:, :T], in0=gr[:, 1:1 + T], scalar=cw(1), in1=tt[:, :T],
                                                            op0=ALU.mult, op1=ALU.add),
                      rd=[b_graw[g], b_vecs, b_tmp[k]], wr=[b_tmp[k]])
                kb.op(DVE, lambda e: e.scalar_tensor_tensor(out=tt[:, :T], in0=gr[:, 0:T], scalar=cw(0), in1=tt[:, :T],
                                                            op0=ALU.mult, op1=ALU.add),
                      rd=[b_graw[g], b_vecs, b_tmp[k]], wr=[b_tmp[k]])
                kb.op(ACT, lambda e: e.activation(out=sil_sb[g][:], in_=tt[:, :T], func=AF.Silu),
                      rd=[b_tmp[k]], wr=[b_sil[g]])
                sil_of[j] = g

            def ev_up(i, pt, bpt, js=js, sil_of=sil_of):
                j = js[i]
                g = sil_of[j]
                kb.op(DVE, lambda e: e.tensor_tensor(out=act3[:, j, :], in0=pt[:, :T], in1=sil_sb[g][:], op=ALU.mult),
                      rd=[bpt, b_sil[g]], wr=[b_act])

            proj(key, lambda k: a_chunk(k), [b_f], DC, [j * 128 for j in js], ev_gate)
            proj(key, lambda k: a_chunk(k), [b_f], DC, [DFF + j * 128 for j in js], ev_up)

        def ev_down(i, pt, bpt):
            kb.op(ACT, lambda e: e.activation(out=f_sb[:, i, :], in_=pt[:, :T], func=AF.Copy), rd=[bpt], wr=[b_f])
        proj(f"w_down{l}", lambda k: act3[:, k, :], [b_act], FC, [m * 128 for m in range(DC)], ev_down)
        residual_add(f"ln_ffn{l}_1")

    def pool_block(it):
        GCC = c.GCC
        rms_stats([h_sb[:, ci, :] for ci in range(DC)], TH, [b_h])
        nbf = GCC * TH * 2
        bufs3 = [act_sb[:, i * nbf:(i + 1) * nbf].bitcast(F32).rearrange("p (c t) -> p c t", t=TH) for i in range(3)]
        d_g = act_sb[:, 3 * nbf:3 * nbf + GCC * T].rearrange("p (c t) -> p c t", t=T)
        a0, bA, bB = bufs3
        for g, win in enumerate(POOL_WINDOWS):
            for cc in range(GCC):
                ci = g * GCC + cc
                kb.op(DVE, lambda e, ci=ci, cc=cc: e.scalar_tensor_tensor(out=a0[:, cc, :], in0=h_sb[:, ci, :],
                                                                          scalar=vcol("ln_mix0_0", ci), in1=rstd_sb[:, :TH],
                                                                          op0=ALU.mult, op1=ALU.mult),
                      rd=[b_h, b_rstd, b_vecs], wr=[b_act])
            src_b = a0
            sh = 1
            pp = [bA, bB]
            pi = 0
            while sh < win:
                dst_b = pp[pi]
                pi ^= 1
                lo = 2 * sh - 1
                kb.op(POOL, lambda e, sh=sh, lo=lo, src_b=src_b, dst_b=dst_b: e.tensor_tensor(
                    out=dst_b[:, :, lo:], in0=src_b[:, :, lo:], in1=src_b[:, :, lo - sh:TH - sh], op=ALU.add),
                      rd=[b_act], wr=[b_act])
                src_b = dst_b
                sh *= 2
            ssum = src_b
            kb.op(DVE, lambda e, win=win, ssum=ssum: e.scalar_tensor_tensor(out=d_g[:, :, :], in0=ssum[:, :, HALO:], scalar=1.0 / win,
                                                                             in1=a0[:, :, HALO:], op0=ALU.mult, op1=ALU.subtract),
                  rd=[b_act], wr=[b_act])
            if it == 0:
                ic = consts[:, CL["invc"] + g * 16: CL["invc"] + (g + 1) * 16]
                for cc in range(GCC):
                    k = state["tmp"]
                    state["tmp"] = (k + 1) % 4
                    kb.op(DVE, lambda e, cc=cc, k=k, ssum=ssum: e.tensor_tensor(out=tmp_sb[k][:, :16], in0=ssum[:, cc, HALO:HALO + 16], in1=ic,
                                                                                op=ALU.mult), rd=[b_act, b_consts], wr=[b_tmp[k]])
                    kb.op(DVE, lambda e, cc=cc, k=k: e.tensor_tensor(out=d_g[:, cc, 0:16], in0=tmp_sb[k][:, :16],
                                                                     in1=a0[:, cc, HALO:HALO + 16], op=ALU.subtract),
                          rd=[b_act, b_tmp[k]], wr=[b_act])

            def evp(i, pt, bpt, g=g):
                ci = g * GCC + i
                kb.op(ACT, lambda e: e.activation(out=f_sb[:, ci, :], in_=pt[:, :T], func=AF.Copy,
                                                  scale=vcol("pool_scale", ci)),
                      rd=[bpt, b_vecs], wr=[b_f])
            proj("pool_w", lambda k: d_g[:, k, :], [b_act], GCC, [m * 128 for m in range(GCC)], evp, r0=g * c.GC)
        residual_add("ln_mix0_1")

    def layer0():
        xa_prepare(0)
        xv = xT.rearrange("(c p) s -> p c s", p=128)
        hv = (hT if depth > 1 else outT).rearrange("(c p) s -> p c s", p=128)

        def tile(t0, first):
            if first:
                kb.op(POOL, lambda e: e.memset(h_sb[:, :, 0:HALO], 0.0), wr=[b_h])
                kb.dma(h_sb[:, :, HALO:], xv[:, :, 0:T], wr=[b_h])
            else:
                kb.dma(h_sb[:, :, :], xv[:, :, bass.ds(t0 - HALO, TH)], wr=[b_h])
            pool_block(0 if first else 1)
            xa_block(0)
            ffn_block(0, first)
            kb.dma(hv[:, :, bass.ds(t0, T)], h_sb[:, :, HALO:], rd=[b_h], wr=[b_hT])
        tile(0, True)
        kb.loop(1, c.NT, lambda i: tile(i * T, False))

    def nsa_layer():
        H, G, J, NS, NCMP = c.H, c.G, c.J, c.NS, c.NCMP
        GW = c.GW
        scale = 128.0 ** -0.5
        slopes = [2.0 ** (-8.0 * (i + 1) / H) for i in range(H)]
        NKT = S // 128
        CCH = (NCMP + 127) // 128
        NCP = CCH * 128
        qT = dscr("qT_s", [D, S], BF16)
        kvT = dscr("kvT_s", [16 * 128, S], BF16)
        vtok = [dscr(f"vtok_s{i}", [S, 512], BF16) for i in range(2)]
        gT = dscr("gT_s", [GW, S], F32)
        oT = dscr("oT_s", [D, S], BF16)
        b_qT, b_kvT, b_vtok, b_gT, b_oT = B("qT"), B("kvT"), B("vtok"), B("gT"), B("oT")
        stage = [sb(f"stage{i}", [128, 512], BF16) for i in range(2)]
        b_stage = [B("st0"), B("st1")]
        gst = sb("gst", [128, T], F32)
        b_gst = B("gst")
        kcmpT = sb("kcmpT", [128, G, NCP], BF16)
        vcmp = sb("vcmp", [128, G, CCH, 128], BF16)
        b_kcmp, b_vcmp = B("kcmp"), B("vcmp")
        posT = sb("posT", [128, 2, 32], BF16)
        b_posT = B("posT")
        cbias = sb("cbias", [128, 4], F32)
        b_cbias = B("cbias")
        hv = hT.rearrange("(c p) s -> p c s", p=128)

        def stg():
            i = state["st"]
            state["st"] ^= 1
            return stage[i], b_stage[i]

        kb.barrier()
        xa_prepare(1)
        HT_ = H * T
        q_stage = act_sb[:, 0:HT_].rearrange("p (h t) -> p h t", t=T)
        kv_stage = act_sb[:, HT_:HT_ + 16 * T].rearrange("p (h t) -> p h t", t=T)
        v_stage = act_sb[:, HT_ + 16 * T:HT_ + 16 * T + 2 * (T // 128) * 512].rearrange("p (b a c) -> p b a c", b=2, c=512)
        b_qst, b_kvst, b_vst = B("qst"), B("kvst"), B("vst")

        def b1_tile(t0):
            kb.dma(h_sb[:, :, HALO:], hv[:, :, bass.ds(t0, T)], rd=[b_hT], wr=[b_h])
            norm_to_a("ln_mix1_0", h_c, [b_h])

            def ev_q(i, pt, bpt):
                kb.op(ACT, lambda e: e.activation(out=q_stage[:, i, :], in_=pt[:, :T], func=AF.Copy), rd=[bpt], wr=[b_qst])
            proj("nsa_w_in", lambda k: a_chunk(k), [b_f], DC, [hd * 128 for hd in range(H)], ev_q)
            kb.dma(qT.rearrange("(h p) s -> p h s", p=128)[:, :, bass.ds(t0, T)], q_stage, rd=[b_qst], wr=[b_qT])
            fm_chunks = [(br * 2 + 0) * G + g for br in range(3) for g in range(G)] + [(0 * 2 + 1) * G + g for g in range(G)]

            def ev_kv(i, pt, bpt):
                kb.op(ACT, lambda e: e.activation(out=kv_stage[:, i, :], in_=pt[:, :T], func=AF.Copy), rd=[bpt], wr=[b_kvst])
            proj("nsa_w_in", lambda k: a_chunk(k), [b_f], DC, [c.QW + ch * 128 for ch in fm_chunks], ev_kv)
            kb.dma(kvT.rearrange("(h p) s -> p h s", p=128)[:, :, bass.ds(t0, T)], kv_stage, rd=[b_kvst], wr=[b_kvT])
            for bi, br in enumerate((1, 2)):
                c0 = c.QW + ((br * 2 + 1) * G) * 128
                for ts_ in range(T // 128):
                    pt, bpt = next_ps()
                    kstep = WSZ // 512
                    for k0 in range(0, DC, kstep):
                        k1 = min(DC, k0 + kstep)
                        wt, bw = load_w("nsa_w_in", k0 * 128, k1 - k0, c0, 512)
                        for k in range(k0, k1):
                            kb.op(PE, lambda e, pt=pt, wt=wt, k=k, k0=k0, ts_=ts_: e.matmul(
                                pt[:, :512], a_chunk(k)[:, ts_ * 128:(ts_ + 1) * 128], wt[:, k - k0, :],
                                start=(k == 0), stop=(k == DC - 1)), rd=[bw, b_f], wr=[bpt])
                    kb.op(ACT, lambda e, pt=pt, bi=bi, ts_=ts_: e.activation(out=v_stage[:, bi, ts_, :], in_=pt[:, :512], func=AF.Copy),
                          rd=[bpt], wr=[b_vst])
                kb.dma(vtok[bi][bass.ds(t0, T), :].rearrange("(a p) c -> p a c", p=128), v_stage[:, bi, :, :], rd=[b_vst], wr=[b_vtok])
            pt, bpt = next_ps()
            kstep = WSZ // GW
            c0 = c.QW + c.KVW
            for k0 in range(0, DC, kstep):
                k1 = min(DC, k0 + kstep)
                wt, bw = load_w("nsa_w_in", k0 * 128, k1 - k0, c0, GW)
                for k in range(k0, k1):
                    kb.op(PE, lambda e, pt=pt, wt=wt, k=k, k0=k0: e.matmul(pt[:GW, :T], wt[:, k - k0, :], a_chunk(k),
                                                                          start=(k == 0), stop=(k == DC - 1)), rd=[bw, b_f], wr=[bpt])
            kb.op(ACT, lambda e, pt=pt: e.activation(out=gst[:GW, :], in_=pt[:GW, :T], func=AF.Sigmoid), rd=[bpt], wr=[b_gst])
            kb.dma(gT[:, bass.ds(t0, T)], gst[:GW, :], rd=[b_gst], wr=[b_gT])

        kb.loop(0, c.NT, lambda i: b1_tile(i * T))
        kb.barrier()
        raw = act_sb[:, 0:S]
        hidT = act_sb[:, S:S + 2048].rearrange("p (a b) -> p a b", b=512)
        xg = [act_sb[:, S + 2048 + i * 1024: S + 2048 + (i + 1) * 1024].bitcast(F32) for i in range(3)]
        b_raw, b_hid = B("raw"), B("hid")
        b_xg = [B("xg0"), B("xg1"), B("xg2")]
        raw3 = raw.rearrange("p (c l) -> p c l", l=16)
        pos_f = tmp_sb[0]
        kb.dma(pos_f[:, 0:64].rearrange("p (a l) -> p a l", l=32), cmp_posT.rearrange("a p l -> p a l"), wr=[b_tmp[0]])
        kb.op(DVE, lambda e: e.tensor_copy(out=posT[:], in_=pos_f[:, 0:64].rearrange("p (a l) -> p a l", l=32)), rd=[b_tmp[0]], wr=[b_posT])
        kb.op(POOL, lambda e: e.memset(hidT, 0.0), wr=[b_hid])
        kb.op(POOL, lambda e: e.memset(kcmpT[:], 0.0), wr=[b_kcmp])
        for kv in range(2):
            w2t, bw2 = None, None
            for g in range(G):
                ch = (0 if kv == 0 else 12) + g
                kb.dma(raw, kvT[ch * 128:(ch + 1) * 128, :], rd=[b_kvT], wr=[b_raw])
                for hc in range(4):
                    wt, bw = load_w("cmp_w1", kv * 4096, 32, hc * 128, 128)
                    pb, bpb = next_ps()
                    for l in range(32):
                        kb.op(PE, lambda e, l=l, wt=wt, pb=pb: e.matmul(pb[:, 0:1], wt[:, l, :], posT[:, kv, l:l + 1],
                                                                       start=(l == 0), stop=(l == 31)), rd=[bw, b_posT], wr=[bpb])
                    kb.op(DVE, lambda e, pb=pb, hc=hc: e.tensor_tensor(out=cbias[:, hc:hc + 1], in0=pb[:, 0:1],
                                                                       in1=vcol(f"cmp_b1_{kv}", hc), op=ALU.add),
                          rd=[bpb, b_vecs], wr=[b_cbias])
                    pt, bpt = next_ps()
                    for l in range(32):
                        rhs = raw3[:, 0:NCMP, l] if l < 16 else raw3[:, 1:NCMP + 1, l - 16]
                        kb.op(PE, lambda e, l=l, wt=wt, pt=pt, rhs=rhs: e.matmul(pt[:, :NCMP], wt[:, l, :], rhs,
                                                                                 start=(l == 0), stop=(l == 31)), rd=[bw, b_raw], wr=[bpt])
                    x0, x1, x2 = xg
                    kb.op(ACT, lambda e, pt=pt, hc=hc: e.activation(out=x0[:, :NCMP], in_=pt[:, :NCMP], func=AF.Identity,
                                                                    bias=cbias[:, hc:hc + 1]), rd=[bpt, b_cbias], wr=[b_xg[0]])
                    kb.op(DVE, lambda e: e.tensor_tensor(out=x1[:, :NCMP], in0=x0[:, :NCMP], in1=x0[:, :NCMP], op=ALU.mult),
                          rd=[b_xg[0]], wr=[b_xg[1]])
                    kb.op(DVE, lambda e: e.tensor_scalar(out=x1[:, :NCMP], in0=x1[:, :NCMP], scalar1=0.044715, scalar2=1.0,
                                                         op0=ALU.mult, op1=ALU.add), rd=[b_xg[1]], wr=[b_xg[1]])
                    kb.op(DVE, lambda e: e.tensor_tensor(out=x1[:, :NCMP], in0=x1[:, :NCMP], in1=x0[:, :NCMP], op=ALU.mult),
                          rd=[b_xg[0], b_xg[1]], wr=[b_xg[1]])
                    kb.op(ACT, lambda e: e.activation(out=x2[:, :NCMP], in_=x1[:, :NCMP], func=AF.Sigmoid, scale=1.5957691216057308),
                          rd=[b_xg[1]], wr=[b_xg[2]])
                    kb.op(DVE, lambda e, hc=hc: e.tensor_tensor(out=hidT[:, hc, :NCMP], in0=x0[:, :NCMP], in1=x2[:, :NCMP], op=ALU.mult),
                          rd=[b_xg[0], b_xg[2]], wr=[b_hid])
                w2t, bw2 = load_w("cmp_w2", kv * 512, 4, 0, 128)
                if kv == 0:
                    pt, bpt = next_ps()
                    for hc in range(4):
                        kb.op(PE, lambda e, hc=hc, pt=pt, w2t=w2t: e.matmul(pt[:, :NCMP], w2t[:, hc, :], hidT[:, hc, :NCMP],
                                                                           start=(hc == 0), stop=(hc == 3)), rd=[bw2, b_hid], wr=[bpt])
                    kb.op(ACT, lambda e, pt=pt, g=g: e.activation(out=kcmpT[:, g, :NCMP], in_=pt[:, :NCMP], func=AF.Copy), rd=[bpt], wr=[b_kcmp])
                else:
                    for cc in range(CCH):
                        pt, bpt = next_ps()
                        for hc in range(4):
                            kb.op(PE, lambda e, hc=hc, pt=pt, w2t=w2t, cc=cc: e.matmul(pt[:, :128], hidT[:, hc, cc * 128:(cc + 1) * 128], w2t[:, hc, :],
                                                                                      start=(hc == 0), stop=(hc == 3)), rd=[bw2, b_hid], wr=[bpt])
                        kb.op(ACT, lambda e, pt=pt, g=g, cc=cc: e.activation(out=vcmp[:, g, cc, :], in_=pt[:, :128], func=AF.Copy), rd=[bpt], wr=[b_vcmp])

        kb.barrier()
        off = [0]

        def carve(n, dt=BF16):
            nb = n if dt == BF16 else 2 * n
            v = act_sb[:, off[0]:off[0] + nb]
            off[0] += (nb + 15) // 16 * 16
            assert off[0] <= ACTN, (off[0], ACTN)
            return v if dt == BF16 else v.bitcast(F32)

        woff = [0]

        def carve_w(n, dt=BF16):
            nb = n if dt == BF16 else 2 * n
            v = w_sb[1][:, woff[0]:woff[0] + nb]
            woff[0] += nb
            assert woff[0] <= WSZ, (woff[0], WSZ)
            return v if dt == BF16 else v.bitcast(F32)

        NWT = 512 // 128 + T // 128 + 1
        kselT = carve(S)
        vsel = carve(S).rearrange("p (k d) -> p k d", d=128)
        ovb = carve(CCH * (NS + 1)).rearrange("p (a n) -> p a n", n=NS + 1)
        selT = carve(T)
        e_sb = [carve(T) for _ in range(3)]
        wmask = [carve(T) for _ in range(NWT)]
        cmask = carve(T)
        tdcl = [carve(T, F32) for _ in range(2)]
        assert NS * 64 <= WSZ
        Ebig = w_sb[0][:, :NS * 64].rearrange("p (n x) -> p n x", x=64)
        kwinT = carve_w(NWT * 128)
        vwin = carve_w(NWT * 128).rearrange("p (k d) -> p k d", d=128)
        qg = carve_w(J * T).rearrange("p (j t) -> p j t", t=T)
        ocmp = carve_w(J * T, F32).rearrange("p (j t) -> p j t", t=T)
        b_ksel, b_vsel, b_kwin, b_vwin, b_qg, b_E, b_ov, b_selT, b_cmask = (B("ksel"), B("vsel"), B("kwin"), B("vwin"), B("qg"), B("E"),
                                                                          B("ov"), B("selT"), B("cmask"))
        b_e = [B("e0"), B("e1"), B("e2")]
        b_wm = [B(f"wm{i}") for i in range(NWT)]
        b_tdcl = [B("tdcl0"), B("tdcl1")]
        b_os = [B(f"os{j}") for j in range(J)]
        need_h = 3 * J * T + 3 * T + 2 * (NS + 1) + 2 * max(NS, 8) + 16 + 8 + NS + T + 64
        if need_h <= DC * TH:
            h_flat = h_sb[:].rearrange("p c t -> p (c t)")
        else:
            h_flat = sb("hx", [128, need_h], F32)[:]
        hoff = [0]

        def carve_h(n):
            v = h_flat[:, hoff[0]:hoff[0] + n]
            hoff[0] += (n + 7) // 8 * 8
            return v
        gates_sb = carve_h(3 * J * T).rearrange("p (b j t) -> p b j t", j=J, t=T)
        tm = [carve_h(T) for _ in range(3)]
        score = carve_h(2 * (NS + 1)).rearrange("p (a n) -> p a n", n=NS + 1)
        adj = carve_h(max(NS, 8))
        work = carve_h(max(NS, 8))
        m8 = carve_h(16)
        dcol = carve_h(8)
        self32 = carve_h(NS)
        rden = carve_h(T)
        b_gates, b_score, b_adj, b_work, b_m8, b_self, b_rden, b_dcol = (B("gates"), B("score"), B("adj"), B("work"), B("m8"),
                                                                        B("self"), B("rden"), B("dcol"))
        b_tm = [B("tm0"), B("tm1"), B("tm2")]
        NKT = S // 128
        assert NKT * T <= 2 * DC * T
        smask = [a_bf[:, i * T:(i + 1) * T] for i in range(NKT)]
        b_sm = [B(f"sm{i}") for i in range(NKT)]
        psc, bpsc = ps[7], b_ps[7]

        Tdk = consts[:, CL["tdk"]:CL["tdk"] + T]
        TdC = consts[:, CL["tdc"]:CL["tdc"] + T]
        ident = consts[:, CL["ident"]:CL["ident"] + 128]
        Tdk_cl = [consts[:, CL["tdk0"]:CL["tdk0"] + T], consts[:, CL["tdk1"]:CL["tdk1"] + T]]

        def td_for(dlt):
            if dlt >= 127:
                return Tdk, dlt
            assert dlt in (0, -128), dlt
            return Tdk_cl[0 if dlt == 0 else 1], 0
        kb.op(DVE, lambda e: e.tensor_copy(out=Ebig[:, :, :], in_=ident[:, 0:NS].unsqueeze(2).to_broadcast([128, NS, 64])),
              rd=[b_consts], wr=[b_E])
        kb.op(DVE, lambda e: e.tensor_copy(out=ovb[:, :, :], in_=consts[:, CL["ov"]:CL["ov"] + CCH * (NS + 1)].rearrange("p (a n) -> p a n", n=NS + 1)),
              rd=[b_consts], wr=[b_ov])
        Eflat = Ebig.rearrange("p n x -> p (n x)")

        def nxt(key, n):
            i = state.get(key, 0)
            state[key] = (i + 1) % n
            return i

        def tile_A(kT_ap, rd_k, j, hd, mask_ap, rd_mask, td, cb, rd_td=()):
            pt, bpt = next_ps()
            kb.op(PE, lambda e: e.matmul(pt[:, :T], kT_ap, qg[:, j, :], start=True, stop=True), rd=rd_k + [b_qg], wr=[bpt])
            ti = nxt("tm", 3)
            kb.op(DVE, lambda e: e.scalar_tensor_tensor(out=tm[ti], in0=td, scalar=-slopes[hd] / scale, in1=pt[:, :T],
                                                        op0=ALU.mult, op1=ALU.add), rd=[bpt, b_consts] + list(rd_td), wr=[b_tm[ti]])
            ei = nxt("e", 3)
            kb.op(ACT, lambda e: e.activation(out=e_sb[ei], in_=tm[ti], func=AF.Exp, scale=scale, bias=float(cb)),
                  rd=[b_tm[ti]], wr=[b_e[ei]])
            if mask_ap is not None:
                kb.op(POOL, lambda e: e.tensor_tensor(out=e_sb[ei], in0=e_sb[ei], in1=mask_ap, op=ALU.mult),
                      rd=[b_e[ei]] + rd_mask, wr=[b_e[ei]])
            return ei

        def tile_B(ei, acc, first, last, v_ap, rd_v):
            pa, bpa = acc
            kb.op(PE, lambda e: e.matmul(pa[:, 0:T], v_ap, e_sb[ei], start=first, stop=last, skip_group_check=True),
                  rd=rd_v + [b_e[ei]], wr=[bpa])
            kb.op(PE, lambda e: e.matmul(pa[:, T:2 * T], ones_bf[:], e_sb[ei], start=False, stop=last, skip_group_check=True),
                  rd=[b_ones, b_e[ei]], wr=[bpa])

        def run_tiles(items, acc, extra_B=None):
            n = len(items)
            LA = 2
            eis = {}
            for idx in range(n + LA):
                if idx < n:
                    it_ = items[idx]
                    eis[idx] = tile_A(*it_["A"])
                k = idx - LA
                if 0 <= k < n:
                    v_ap, rd_v = items[k]["V"]
                    tile_B(eis[k], acc, k == 0, k == n - 1, v_ap, rd_v)
                    if extra_B is not None:
                        extra_B(k, eis[k])

        def finish_branch(acc, br, j, first_branch):
            pa, bpa = acc
            kb.op(DVE, lambda e: e.tensor_scalar(out=rden, in0=pa[:, T:2 * T], scalar1=1e-30, scalar2=None, op0=ALU.max),
                  rd=[bpa], wr=[b_rden])
            kb.op(DVE, lambda e: e.reciprocal(out=rden, in_=rden), rd=[b_rden], wr=[b_rden])
            kb.op(DVE, lambda e: e.tensor_tensor(out=rden, in0=rden, in1=gates_sb[:, br, j, :], op=ALU.mult),
                  rd=[b_rden, b_gates], wr=[b_rden])
            if first_branch:
                kb.op(DVE, lambda e: e.tensor_tensor(out=ocmp[:, j, :], in0=pa[:, 0:T], in1=rden, op=ALU.mult), rd=[bpa, b_rden], wr=[b_os[j]])
            else:
                ti = nxt("tm", 3)
                kb.op(DVE, lambda e: e.tensor_tensor(out=tm[ti], in0=pa[:, 0:T], in1=rden, op=ALU.mult), rd=[bpa, b_rden], wr=[b_tm[ti]])
                kb.op(POOL, lambda e: e.tensor_tensor(out=ocmp[:, j, :], in0=ocmp[:, j, :], in1=tm[ti], op=ALU.add), rd=[b_tm[ti], b_os[j]], wr=[b_os[j]])

        for g in range(G):
            kb.dma(kselT, kvT[(4 + g) * 128:(4 + g + 1) * 128, :], rd=[b_kvT], wr=[b_ksel])
            kb.dma(vsel, vtok[0][:, g * 128:(g + 1) * 128].rearrange("(k p) d -> p k d", p=128), rd=[b_vtok], wr=[b_vsel])
            for it in range(c.NT):
                t0 = it * T
                tmax = t0 + T - 1
                kb.dma(qg, qT[g * J * 128:(g + 1) * J * 128, t0:t0 + T].rearrange("(j p) t -> p j t", p=128), rd=[b_qT], wr=[b_qg])
                for br in range(3):
                    r0 = br * H + g * J
                    kb.dma(gates_sb[:, br, :, :], gT[r0:r0 + J, t0:t0 + T].partition_broadcast(128), rd=[b_gT], wr=[b_gates])
                kt_lo = max(0, (t0 - 511) // 128)
                kt_hi = tmax // 128
                nw = kt_hi - kt_lo + 1
                kb.dma(kwinT[:, :nw * 128], kvT[(8 + g) * 128:(8 + g + 1) * 128, kt_lo * 128:(kt_hi + 1) * 128],
                       rd=[b_kvT], wr=[b_kwin])
                kb.dma(vwin[:, :nw, :], vtok[1][kt_lo * 128:(kt_hi + 1) * 128, g * 128:(g + 1) * 128].rearrange("(k p) d -> p k d", p=128),
                       rd=[b_vtok], wr=[b_vwin])
                for wi in range(nw):
                    dlt = t0 - (kt_lo + wi) * 128
                    kb.op(DVE, lambda e, wi=wi, dlt=dlt: e.tensor_scalar(out=wmask[wi], in0=Tdk, scalar1=float(dlt), scalar2=0.0,
                                                                         op0=ALU.add, op1=ALU.is_ge), rd=[b_consts], wr=[b_wm[wi]])
                    kb.op(DVE, lambda e, wi=wi, dlt=dlt: e.tensor_scalar(out=cmask, in0=Tdk, scalar1=float(dlt), scalar2=512.0,
                                                                         op0=ALU.add, op1=ALU.is_lt), rd=[b_consts], wr=[b_cmask])
                    kb.op(DVE, lambda e, wi=wi: e.tensor_tensor(out=wmask[wi], in0=wmask[wi], in1=cmask, op=ALU.mult),
                          rd=[b_cmask, b_wm[wi]], wr=[b_wm[wi]])
                cmax = (tmax - 31) // 16
                nch = 0 if cmax < 0 else cmax // 128 + 1
                kb.op(POOL, lambda e: e.memset(score, 0.0), wr=[b_score])
                cm = {}
                for cc in range(nch):
                    offc = t0 - 2048 * cc - 31
                    if offc - 16 * 127 < 0:
                        si = NKT - 1 - (len(cm) % 2)
                        ci_ = len(cm) % 2
                        mt = smask[si]
                        kb.op(DVE, lambda e, mt=mt, offc=offc: e.tensor_scalar(out=mt, in0=TdC, scalar1=float(offc), scalar2=0.0,
                                                                               op0=ALU.add, op1=ALU.is_ge), rd=[b_consts], wr=[b_sm[si]])
                        kb.op(DVE, lambda e, ci_=ci_, offc=offc: e.tensor_scalar(out=tdcl[ci_], in0=TdC, scalar1=float(offc), scalar2=0.0,
                                                                                 op0=ALU.add, op1=ALU.max), rd=[b_consts], wr=[b_tdcl[ci_]])
                        cm[cc] = (mt, b_sm[si], tdcl[ci_], b_tdcl[ci_])
                assert len(cm) <= 2
                for j in range(J):
                    hd = g * J + j
                    if nch == 0:
                        kb.op(POOL, lambda e, j=j: e.memset(ocmp[:, j, :], 0.0), wr=[b_os[j]])
                        continue
                    acc = next_acc()
                    items = []
                    for cc in range(nch):
                        offc = t0 - 2048 * cc - 31
                        m = cm.get(cc)
                        if m:
                            A = (kcmpT[:, g, cc * 128:(cc + 1) * 128], [b_kcmp], j, hd, m[0], [m[1]], m[2], 0.0, [m[3]])
                        else:
                            A = (kcmpT[:, g, cc * 128:(cc + 1) * 128], [b_kcmp], j, hd, None, [], TdC, -slopes[hd] * offc)
                        items.append({"A": A, "V": (vcmp[:, g, cc, :], [b_vcmp])})

                    def extra(cc, ei, nch=nch):
                        for hf in range(T // 128):
                            kb.op(PE, lambda e, ei=ei, hf=hf, cc=cc: e.matmul(psc[:, hf * (NS + 1):(hf + 1) * (NS + 1)],
                                                                             e_sb[ei][:, hf * 128:(hf + 1) * 128], ovb[:, cc, :],
                                                                             start=(cc == 0 and hf == 0), stop=(cc == nch - 1),
                                                                             skip_group_check=True), rd=[b_e[ei], b_ov], wr=[bpsc])
                    run_tiles(items, acc, extra)
                    finish_branch(acc, 0, j, True)
                    for hf in range(T // 128):
                        sc_ps = psc[:, hf * (NS + 1):(hf + 1) * (NS + 1)]
                        kb.op(DVE, lambda e, sc_ps=sc_ps: e.tensor_scalar(out=dcol[:, 0:1], in0=sc_ps[:, NS:NS + 1], scalar1=1e-30, scalar2=None,
                                                                          op0=ALU.max), rd=[bpsc], wr=[b_dcol])
                        kb.op(DVE, lambda e: e.reciprocal(out=dcol[:, 0:1], in_=dcol[:, 0:1]), rd=[b_dcol], wr=[b_dcol])
                        kb.op(DVE, lambda e, hf=hf, sc_ps=sc_ps: e.scalar_tensor_tensor(out=score[:, hf, :], in0=sc_ps, scalar=dcol[:, 0:1],
                                                                                        in1=score[:, hf, :], op0=ALU.mult, op1=ALU.add),
                              rd=[bpsc, b_dcol, b_score], wr=[b_score])
                for hf in range(T // 128):
                    cur0 = (t0 + hf * 128) // 64
                    nf = consts[:, CL["bnf"] + NS - cur0: CL["bnf"] + 2 * NS - cur0]
                    ad = consts[:, CL["badd"] + NS - cur0: CL["badd"] + 2 * NS - cur0]
                    kb.op(DVE, lambda e, hf=hf, nf=nf: e.tensor_tensor(out=adj[:, :NS], in0=score[:, hf, :NS], in1=nf, op=ALU.mult),
                          rd=[b_score, b_consts], wr=[b_adj])
                    kb.op(DVE, lambda e, ad=ad: e.tensor_tensor(out=adj[:, :NS], in0=adj[:, :NS], in1=ad, op=ALU.add), rd=[b_adj, b_consts], wr=[b_adj])
                    kb.op(DVE, lambda e: e.memset(adj[:, 0:1], 1e6), rd=[b_adj], wr=[b_adj])
                    if NS > 16:
                        kb.op(DVE, lambda e: e.max(out=m8[:, 0:8], in_=adj[:, :NS]), rd=[b_adj], wr=[b_m8])
                        kb.op(DVE, lambda e: e.match_replace(out=work[:, :NS], in_to_replace=m8[:, 0:8], in_values=adj[:, :NS], imm_value=-1e30),
                              rd=[b_adj, b_m8], wr=[b_work])
                        kb.op(DVE, lambda e: e.max(out=m8[:, 0:8], in_=work[:, :NS]), rd=[b_work], wr=[b_m8])
                        kb.op(DVE, lambda e: e.tensor_scalar(out=self32[:, :NS], in0=adj[:, :NS], scalar1=m8[:, 7:8], scalar2=None, op0=ALU.is_ge),
                              rd=[b_adj, b_m8], wr=[b_self])
                    else:
                        kb.op(DVE, lambda e: e.memset(self32[:, :NS], 1.0), wr=[b_self])
                    ptr, bptr = next_ps()
                    kb.op(PE, lambda e, ptr=ptr: e.transpose(ptr[:NS, :128], self32[:, :NS], ident), rd=[b_self, b_consts], wr=[bptr])
                    kb.op(ACT, lambda e, ptr=ptr, hf=hf: e.activation(out=selT[:NS, hf * 128:(hf + 1) * 128], in_=ptr[:NS, :128], func=AF.Copy),
                          rd=[bptr], wr=[b_selT])
                nkt = tmax // 128 + 1
                for kt in range(nkt):
                    pm, bpm = next_ps()
                    kb.op(PE, lambda e, kt=kt, pm=pm: e.matmul(pm[:, :T], Eflat[:NS, kt * 128:(kt + 1) * 128], selT[:NS, :], start=True, stop=True),
                          rd=[b_E, b_selT], wr=[bpm])
                    kb.op(ACT, lambda e, kt=kt, pm=pm: e.activation(out=smask[kt], in_=pm[:, :T], func=AF.Copy), rd=[bpm], wr=[b_sm[kt]])
                    if kt * 128 + 127 > t0:
                        dlt = t0 - kt * 128
                        kb.op(DVE, lambda e, dlt=dlt: e.tensor_scalar(out=cmask, in0=Tdk, scalar1=float(dlt), scalar2=0.0,
                                                                      op0=ALU.add, op1=ALU.is_ge), rd=[b_consts], wr=[b_cmask])
                        kb.op(DVE, lambda e, kt=kt: e.tensor_tensor(out=smask[kt], in0=smask[kt], in1=cmask, op=ALU.mult),
                              rd=[b_cmask, b_sm[kt]], wr=[b_sm[kt]])
                for j in range(J):
                    hd = g * J + j
                    acc = next_acc()
                    items = []
                    for kt in range(nkt):
                        td, cbm = td_for(t0 - kt * 128)
                        items.append({"A": (kselT[:, kt * 128:(kt + 1) * 128], [b_ksel], j, hd, smask[kt], [b_sm[kt]], td, -slopes[hd] * cbm),
                                      "V": (vsel[:, kt, :], [b_vsel])})
                    run_tiles(items, acc)
                    finish_branch(acc, 1, j, False)
                    acc = next_acc()
                    items = []
                    for wi in range(nw):
                        kt = kt_lo + wi
                        td, cbm = td_for(t0 - kt * 128)
                        items.append({"A": (kwinT[:, wi * 128:(wi + 1) * 128], [b_kwin], j, hd, wmask[wi], [b_wm[wi]], td, -slopes[hd] * cbm),
                                      "V": (vwin[:, wi, :], [b_vwin])})
                    run_tiles(items, acc)
                    finish_branch(acc, 2, j, False)
                    st, bst = stg()
                    kb.op(ACT, lambda e, st=st, j=j: e.activation(out=st[:, :T], in_=ocmp[:, j, :], func=AF.Copy), rd=[b_os[j]], wr=[bst])
                    kb.dma(oT[hd * 128:(hd + 1) * 128, t0:t0 + T], st[:, :T], rd=[bst], wr=[b_oT])

        kb.barrier()
        ov_ = oT.rearrange("(c p) s -> p c s", p=128)
        outv = outT.rearrange("(c p) s -> p c s", p=128)
        o_in = act_sb[:, :DC * T].rearrange("p (c t) -> p c t", t=T)
        b_outd = B("outdummy")

        def b4_tile(t0, first):
            kb.dma(h_sb[:, :, HALO:], hv[:, :, bass.ds(t0, T)], rd=[b_hT], wr=[b_h])
            kb.dma(o_in, ov_[:, :, bass.ds(t0, T)], rd=[b_oT], wr=[b_act])

            def ev_o(i, pt, bpt):
                kb.op(ACT, lambda e: e.activation(out=f_sb[:, i, :], in_=pt[:, :T], func=AF.Copy), rd=[bpt], wr=[b_f])
            proj("nsa_w_out", lambda k: o_in[:, k, :], [b_act], DC, [m * 128 for m in range(DC)], ev_o)
            residual_add("ln_mix1_1")
            xa_block(1)
            ffn_block(1, first)
            kb.dma(outv[:, :, bass.ds(t0, T)], h_sb[:, :, HALO:], rd=[b_h], wr=[b_outd])
        b4_tile(0, True)
        kb.loop(1, c.NT, lambda i: b4_tile(i * T, False))

    layer0()
    if depth > 1:
        nsa_layer()
    kb.finish()
    return nc, es


def vec_layout(c):
    L = {}
    n = 0

    def add(name, w):
        nonlocal n
        L[name] = n
        n += w
    for l in range(c.depth):
        for nm in ("ln_mix", "ln_xa", "ln_ffn"):
            for i in range(2):
                add(f"{nm}{l}_{i}", c.DC)
        for r in range(3):
            add(f"conv_w{l}_{r}", c.FC)
        add(f"conv_b{l}", c.FC)
    add("mem_norm", c.DC)
    add("pool_scale", c.DC)
    add("cmp_b1_0", 4)
    add("cmp_b1_1", 4)
    L["_n"] = n
    return L


def const_layout(c):
    L = {}
    n = 0

    def add(name, w):
        nonlocal n
        L[name] = n
        n += w
    add("ones", 128)
    add("ident", 128)
    add("invc", 64)
    add("tdk", c.T)
    add("tdc", c.T)
    add("tdk0", c.T)
    add("tdk1", c.T)
    add("ov", ((c.NCMP + 127) // 128) * (c.NS + 1))
    add("bnf", 2 * c.NS)
    add("badd", 2 * c.NS)
    L["_n"] = n
    return L


def fm(v):
    v = np.asarray(v, np.float32)
    return np.ascontiguousarray(v.reshape(-1, 128).T)


def make_inputs(c, b, inp):
    VL = vec_layout(c)
    vecs = np.zeros((128, VL["_n"]), np.float32)

    def put(name, v):
        a = fm(v)
        vecs[:, VL[name]:VL[name] + a.shape[1]] = a
    for l in range(c.depth):
        for nm in ("ln_mix", "ln_xa", "ln_ffn"):
            for i in range(2):
                put(f"{nm}{l}_{i}", inp[nm][l, i])
        for r in range(3):
            put(f"conv_w{l}_{r}", inp["ffn_conv_w"][l, r])
        put(f"conv_b{l}", inp["ffn_conv_b"][l])
    put("mem_norm", inp["mem_norm"])
    put("pool_scale", inp["pool_scale"][0])
    if c.depth > 1:
        put("cmp_b1_0", inp["nsa_cmp_b1"][0, 0])
        put("cmp_b1_1", inp["nsa_cmp_b1"][0, 1])
    CL = const_layout(c)
    consts = np.zeros((128, CL["_n"]), np.float32)
    consts[:, CL["ones"]:CL["ones"] + 128] = 1.0
    consts[:, CL["ident"]:CL["ident"] + 128] = np.eye(128, dtype=np.float32)
    for g, win in enumerate(POOL_WINDOWS):
        consts[:, CL["invc"] + g * 16: CL["invc"] + (g + 1) * 16] = 1.0 / np.minimum(np.arange(16) + 1, win)
    T = c.T
    ql = np.arange(T)[None, :].astype(np.float32)
    kl = np.arange(128)[:, None].astype(np.float32)
    consts[:, CL["tdk"]:CL["tdk"] + T] = ql - kl
    consts[:, CL["tdc"]:CL["tdc"] + T] = ql - 16 * kl
    consts[:, CL["tdk0"]:CL["tdk0"] + T] = np.maximum(ql - kl, 0)
    consts[:, CL["tdk1"]:CL["tdk1"] + T] = np.maximum(ql - kl - 128, 0)
    CCH = (c.NCMP + 127) // 128
    NS = c.NS
    ov = np.zeros((128, CCH, NS + 1), np.float32)
    for cc in range(CCH):
        cidx = cc * 128 + np.arange(128)
        n = np.arange(NS)
        o = ((cidx[:, None] >= 4 * n[None, :] - 1) & (cidx[:, None] <= 4 * n[None, :] + 3) & (cidx[:, None] < c.NCMP))
        ov[:, cc, :NS] = o
        ov[:, cc, NS] = 1.0
    consts[:, CL["ov"]:CL["ov"] + CCH * (NS + 1)] = ov.reshape(128, -1)
    qb = (np.arange(128) // 64)[:, None]
    xx = np.arange(2 * NS)[None, :] - NS
    nf = (xx <= qb).astype(np.float32)
    ff = ((xx == qb) | (xx == qb - 1)).astype(np.float32)
    consts[:, CL["bnf"]:CL["bnf"] + 2 * NS] = nf
    consts[:, CL["badd"]:CL["badd"] + 2 * NS] = nf - 1.0 + 1e6 * ff
    m = {
        "xT": np.ascontiguousarray(np.asarray(inp["x"][b], np.float32).T),
        "memT": np.ascontiguousarray(np.asarray(inp["mem"][b], np.float32).T),
        "vecs": vecs,
        "consts": consts,
        "pool_w": np.ascontiguousarray(np.asarray(inp["pool_w"][0], np.float32).reshape(4 * c.GC, c.GC)),
    }
    for l in range(c.depth):
        m[f"xa_wq{l}"] = np.asarray(inp["xa_wq"][l], np.float32)
        m[f"xa_wkv{l}"] = np.asarray(inp["xa_wkv"][l], np.float32)
        m[f"xa_wo{l}"] = np.asarray(inp["xa_wo"][l], np.float32)
        m[f"w_gu{l}"] = np.asarray(inp["ffn_w_gu"][l], np.float32)
        m[f"w_down{l}"] = np.asarray(inp["ffn_w_down"][l], np.float32)
    if c.depth > 1:
        m["nsa_w_in"] = np.asarray(inp["nsa_w_in"][0], np.float32)
        m["nsa_w_out"] = np.asarray(inp["nsa_w_out"][0], np.float32)
        m["cmp_posT"] = np.ascontiguousarray(np.asarray(inp["nsa_cmp_pos"][0], np.float32).transpose(0, 2, 1))
        m["cmp_w1"] = np.ascontiguousarray(np.asarray(inp["nsa_cmp_w1"][0], np.float32).reshape(2 * 4096, 512))
        m["cmp_w2"] = np.ascontiguousarray(np.asarray(inp["nsa_cmp_w2"][0], np.float32).reshape(2 * 512, 128))
    return m


def run(c, inp, n_layers=None):
    nc, es = build(c, n_layers=n_layers)
    Bn = inp["x"].shape[0]
    in_maps = [make_inputs(c, b, inp) for b in range(Bn)]
    with es:
        pass
    res = run_bass_kernel_spmd(nc, in_maps, core_ids=list(range(Bn)))
    out = np.stack([np.ascontiguousarray(res.results[b]["outT"].T) for b in range(Bn)])
    return out


def kernel(**inputs):
    c = Cfg()
    return run(c, inputs).astype(np.float32)
```
